# Optimizing a Trainium2 kernel written in Bass

```python
import math
import jax, jax.numpy as jnp
from jax import lax
import numpy as np

D_MODEL = 1024
BATCH = 8
SEQ = 2048
DEPTH = 4

CTX_LEN = 256
GRID_W = 64
EPS = 1e-6
NEG = -1e30
F32 = jnp.float32
N_MOD = 9

D_FF = 2816

D_RNN = 512
RNN_BLOCKS = 8
RNN_BLOCK = D_RNN // RNN_BLOCKS
CONV_W = 4
CONV_LEFT = 2
LRU_C = 8.0

ML_HEADS = 4
ML_DH = 128
ML_W = ML_HEADS * ML_DH
ML_CHUNK = 128

AT_HEADS = 8
AT_KV = 2
AT_DH = 64
AT_W = AT_HEADS * AT_DH
AT_KVW = AT_KV * AT_DH
WINDOW = 128
ATT_BLOCK = 128
ROPE_BASE = 10000.0

N_BRANCH = 3
BRANCH_W = 512

OFF_RG_X = 0
OFF_RG_G = OFF_RG_X + D_RNN
OFF_ML_Q = OFF_RG_G + D_RNN
OFF_ML_K = OFF_ML_Q + ML_W
OFF_ML_V = OFF_ML_K + ML_W
OFF_ML_O = OFF_ML_V + ML_W
OFF_ML_G = OFF_ML_O + ML_W
OFF_AT_Q = OFF_ML_G + 4 * ML_HEADS
OFF_AT_K = OFF_AT_Q + AT_W
OFF_AT_V = OFF_AT_K + AT_KVW
OFF_BR_G = OFF_AT_V + AT_KVW
D_IN = OFF_BR_G + N_BRANCH * D_MODEL

kernel_name = 'hybrid_rglru_mlstm_swa_prefix_dit'


def rmsnorm(x, g):
    xf = x.astype(F32)
    y = xf * lax.rsqrt(jnp.mean(xf * xf, axis=-1, keepdims=True) + EPS)
    return (y * g.astype(F32)).astype(x.dtype)


def ln_mod(x, g, shift, scale):
    return rmsnorm(x, g) * (1 + scale) + shift


def swiglu(h, w1, w3, w2):
    return (jax.nn.silu(h @ w1) * (h @ w3)) @ w2


def axial_angles(L):
    rows = L // GRID_W
    row = jnp.repeat(jnp.arange(rows), GRID_W).astype(F32)
    col = jnp.broadcast_to(jnp.arange(GRID_W), (rows, GRID_W)).reshape(-1).astype(F32)
    half = AT_DH // 2
    inv = ROPE_BASE ** (-jnp.arange(0, half, 2, dtype=F32) / half)
    ar = row[:, None] * inv
    ac = col[:, None] * inv
    ang = jnp.concatenate([ar, ar, ac, ac], axis=-1)
    return jnp.cos(ang), jnp.sin(ang)


def rotate_axial(x):
    a1, a2, b1, b2 = jnp.split(x, 4, axis=-1)
    return jnp.concatenate([-a2, a1, -b2, b1], axis=-1)


def apply_rope(x, cos, sin):
    return x * cos[:, None, :] + rotate_axial(x) * sin[:, None, :]


def dwconv(x, w, b):
    L = x.shape[1]
    xp = jnp.pad(x, ((0, 0), (CONV_LEFT, CONV_W - 1 - CONV_LEFT), (0, 0)))
    out = b
    for k in range(CONV_W):
        out = out + xp[:, k:k + L] * w[k]
    return out


def rglru_coeffs(u, wa, ba, wi, bi, lam):
    B, L, _ = u.shape
    ub = u.reshape(B, L, RNN_BLOCKS, RNN_BLOCK)
    r = jax.nn.sigmoid(jnp.einsum('blni,nij->blnj', ub, wa.astype(F32)).reshape(B, L, D_RNN) + ba.astype(F32))
    i = jax.nn.sigmoid(jnp.einsum('blni,nij->blnj', ub, wi.astype(F32)).reshape(B, L, D_RNN) + bi.astype(F32))
    log_a = -LRU_C * jax.nn.softplus(-lam.astype(F32)) * r
    a = jnp.exp(log_a)
    b = jnp.sqrt(-jnp.expm1(2.0 * log_a)) * (i * u)
    return a, b


def linear_scan(a, b, h0):
    b = b.at[:, 0].add(a[:, 0] * h0)
    def comb(lft, rgt):
        return lft[0] * rgt[0], rgt[0] * lft[1] + rgt[1]
    _, h = lax.associative_scan(comb, (a, b), axis=1)
    return h


def rglru_branch(pl, pc, conv_w, conv_b, wa, ba, wi, bi, lam, ctx_out):
    ul = dwconv(pl[..., OFF_RG_X:OFF_RG_G], conv_w, conv_b).astype(F32)
    uc = dwconv(pc[..., OFF_RG_X:OFF_RG_G], conv_w, conv_b).astype(F32)
    hl = 0.0
    hc = 0.0
    for d in range(2):
        al, bl = rglru_coeffs(ul, wa[d], ba[d], wi[d], bi[d], lam[d])
        ac, bc = rglru_coeffs(uc, wa[d], ba[d], wi[d], bi[d], lam[d])
        if d == 1:
            al, bl, ac, bc = (jnp.flip(z, axis=1) for z in (al, bl, ac, bc))
        h_c = linear_scan(ac, bc, jnp.zeros_like(ac[:, 0]))
        h_l = linear_scan(al, bl, h_c[:, -1])
        if d == 1:
            h_l, h_c = jnp.flip(h_l, axis=1), jnp.flip(h_c, axis=1)
        hl = hl + h_l
        hc = hc + h_c
    yl = (hl * jax.nn.gelu(pl[..., OFF_RG_G:OFF_ML_Q].astype(F32))).astype(pl.dtype)
    yc = (hc * jax.nn.gelu(pc[..., OFF_RG_G:OFF_ML_Q].astype(F32))).astype(pc.dtype) if ctx_out else None
    return yl, yc


def mlstm_scan(q, k, v, li, lf, state, with_out):
    B, H, T, Dh = q.shape
    nc = T // ML_CHUNK
    def chunks(z):
        return jnp.moveaxis(z.reshape(B, H, nc, ML_CHUNK, *z.shape[3:]), 2, 0)
    causal = jnp.tril(jnp.ones((ML_CHUNK, ML_CHUNK), dtype=bool))
    def step(carry, inp):
        C, n, m = carry
        qc, kc, vc, ic, fc = inp
        b = jnp.cumsum(fc, axis=-1)
        b_end = b[..., -1]
        w_end = b_end[..., None] - b + ic
        m_new = jnp.maximum(b_end + m, jnp.max(w_end, axis=-1))
        carry_decay = jnp.exp(b_end + m - m_new)
        w = jnp.exp(w_end - m_new[..., None])
        C_new = carry_decay[..., None, None] * C + jnp.einsum('bhs,bhsk,bhsv->bhkv', w, kc, vc)
        n_new = carry_decay[..., None] * n + jnp.einsum('bhs,bhsk->bhk', w, kc)
        if not with_out:
            return (C_new, n_new, m_new), None
        log_d = jnp.where(causal, b[..., :, None] - b[..., None, :] + ic[..., None, :], NEG)
        inter = b + m[..., None]
        m_t = jnp.maximum(inter, jnp.max(log_d, axis=-1))
        s = jnp.einsum('bhtd,bhsd->bhts', qc, kc) * jnp.exp(log_d - m_t[..., None])
        dec = jnp.exp(inter - m_t)
        num = jnp.einsum('bhts,bhsv->bhtv', s, vc) + dec[..., None] * jnp.einsum('bhtk,bhkv->bhtv', qc, C)
        den = jnp.sum(s, axis=-1) + dec * jnp.einsum('bhtk,bhk->bht', qc, n)
        h = num / jnp.maximum(jnp.abs(den), jnp.exp(-m_t))[..., None]
        return (C_new, n_new, m_new), h
    state, ys = lax.scan(step, state, (chunks(q), chunks(k), chunks(v), chunks(li), chunks(lf)))
    if not with_out:
        return state, None
    return state, jnp.moveaxis(ys, 0, 2).reshape(B, H, T, Dh)


def heads(z, h, d):
    return jnp.swapaxes(z.reshape(z.shape[0], z.shape[1], h, d), 1, 2)


def mlstm_branch(pl, pc, gate_b, norm_g, ctx_out):
    B = pl.shape[0]
    def prep(p):
        q = heads(p[..., OFF_ML_Q:OFF_ML_K], ML_HEADS, ML_DH).astype(F32)
        k = heads(p[..., OFF_ML_K:OFF_ML_V], ML_HEADS, ML_DH).astype(F32) * (ML_DH ** -0.5)
        v = heads(p[..., OFF_ML_V:OFF_ML_O], ML_HEADS, ML_DH).astype(F32)
        g = p[..., OFF_ML_G:OFF_AT_Q].astype(F32) + gate_b.astype(F32)
        g = jnp.moveaxis(g.reshape(p.shape[0], p.shape[1], 4, ML_HEADS), 1, -1)
        return q, k, v, g
    ql, kl, vl, gl = prep(pl)
    qc, kc, vc, gc = prep(pc)
    zero = (jnp.zeros((B, ML_HEADS, ML_DH, ML_DH), F32), jnp.zeros((B, ML_HEADS, ML_DH), F32),
            jnp.zeros((B, ML_HEADS), F32))
    hl = 0.0
    hc = 0.0
    for d in range(2):
        seq_l = (ql, kl, vl, gl[:, 2 * d], jax.nn.log_sigmoid(gl[:, 2 * d + 1]))
        seq_c = (qc, kc, vc, gc[:, 2 * d], jax.nn.log_sigmoid(gc[:, 2 * d + 1]))
        if d == 1:
            seq_l = tuple(jnp.flip(z, axis=2) for z in seq_l)
            seq_c = tuple(jnp.flip(z, axis=2) for z in seq_c)
        st, out_c = mlstm_scan(*seq_c, zero, ctx_out)
        _, out_l = mlstm_scan(*seq_l, st, True)
        if d == 1:
            out_l = jnp.flip(out_l, axis=2)
            out_c = jnp.flip(out_c, axis=2) if ctx_out else None
        hl = hl + out_l
        if ctx_out:
            hc = hc + out_c
    def finish(h, p):
        h = h * lax.rsqrt(jnp.mean(h * h, axis=-1, keepdims=True) + EPS)
        h = jnp.swapaxes(h, 1, 2).reshape(B, -1, ML_W) * norm_g.astype(F32)
        return (h * jax.nn.sigmoid(p[..., OFF_ML_O:OFF_ML_G].astype(F32))).astype(p.dtype)
    yl = finish(hl, pl)
    yc = finish(hc, pc) if ctx_out else None
    return yl, yc


def attention_branch(pl, pc, qn_g, kn_g, sink, cos, sin, ctx_out):
    B, L, _ = pl.shape
    Lc = pc.shape[1]
    G = AT_HEADS // AT_KV
    nb = L // ATT_BLOCK
    nk = 3 * ATT_BLOCK
    scale = AT_DH ** -0.5
    def qkv(p):
        q = rmsnorm(p[..., OFF_AT_Q:OFF_AT_K].reshape(B, -1, AT_HEADS, AT_DH), qn_g).astype(F32)
        k = rmsnorm(p[..., OFF_AT_K:OFF_AT_V].reshape(B, -1, AT_KV, AT_DH), kn_g).astype(F32)
        v = p[..., OFF_AT_V:OFF_BR_G].reshape(B, -1, AT_KV, AT_DH).astype(F32)
        return q, k, v
    ql, kl, vl = qkv(pl)
    qc, kc, vc = qkv(pc)
    ql = apply_rope(ql, cos, sin)
    kl = apply_rope(kl, cos, sin)
    sink_hg = sink.astype(F32).reshape(AT_KV, G)
    qb = ql.reshape(B, nb, ATT_BLOCK, AT_KV, G, AT_DH) * scale
    def band(z):
        zp = jnp.pad(z, ((0, 0), (ATT_BLOCK, ATT_BLOCK), (0, 0), (0, 0)))
        zp = zp.reshape(B, nb + 2, ATT_BLOCK, AT_KV, AT_DH)
        return jnp.concatenate([zp[:, :-2], zp[:, 1:-1], zp[:, 2:]], axis=2)
    kband, vband = band(kl), band(vl)
    r = jnp.arange(nk)
    i = jnp.arange(ATT_BLOCK)
    in_win = jnp.abs(r[None, :] - ATT_BLOCK - i[:, None]) <= WINDOW
    kblk = jnp.arange(nb)[:, None] - 1 + r[None, :] // ATT_BLOCK
    in_rng = (kblk >= 0) & (kblk < nb)
    mask = in_win[None] & in_rng[:, None, :]
    s_lat = jnp.einsum('bnqhgd,bnkhd->bnhgqk', qb, kband)
    s_lat = jnp.where(mask[None, :, None, None], s_lat, NEG)
    s_ctx = jnp.einsum('bnqhgd,bchd->bnhgqc', qb, kc)
    s_snk = jnp.broadcast_to(sink_hg[None, None, :, :, None, None], s_lat.shape[:-1] + (1,))
    p = jax.nn.softmax(jnp.concatenate([s_lat, s_ctx, s_snk], axis=-1), axis=-1)
    o = (jnp.einsum('bnhgqk,bnkhd->bnqhgd', p[..., :nk], vband)
         + jnp.einsum('bnhgqc,bchd->bnqhgd', p[..., nk:nk + Lc], vc))
    yl = o.reshape(B, L, AT_W).astype(pl.dtype)
    yc = None
    if ctx_out:
        qcg = qc.reshape(B, Lc, AT_KV, G, AT_DH) * scale
        s = jnp.einsum('bqhgd,bchd->bhgqc', qcg, kc)
        s_snk_c = jnp.broadcast_to(sink_hg[None, :, :, None, None], s.shape[:-1] + (1,))
        pc_ = jax.nn.softmax(jnp.concatenate([s, s_snk_c], axis=-1), axis=-1)
        oc = jnp.einsum('bhgqc,bchd->bqhgd', pc_[..., :Lc], vc)
        yc = oc.reshape(B, Lc, AT_W).astype(pc.dtype)
    return yl, yc


def token_mixer(hl, hc, w_in, rg_conv_w, rg_conv_b, rg_wa, rg_ba, rg_wi, rg_bi, rg_lam,
                ml_gate_b, ml_norm_g, at_qn_g, at_kn_g, at_sink, w_branch, w_out, cos, sin, ctx_out):
    pl = hl @ w_in
    pc = hc @ w_in
    ya_l, ya_c = rglru_branch(pl, pc, rg_conv_w, rg_conv_b, rg_wa, rg_ba, rg_wi, rg_bi, rg_lam, ctx_out)
    yb_l, yb_c = mlstm_branch(pl, pc, ml_gate_b, ml_norm_g, ctx_out)
    yc_l, yc_c = attention_branch(pl, pc, at_qn_g, at_kn_g, at_sink, cos, sin, ctx_out)
    def merge(p, ya, yb, yc):
        g = jax.nn.sigmoid(p[..., OFF_BR_G:]).reshape(p.shape[:-1] + (N_BRANCH, D_MODEL))
        m = (g[..., 0, :] * (ya @ w_branch[0]) + g[..., 1, :] * (yb @ w_branch[1])
             + g[..., 2, :] * (yc @ w_branch[2]))
        return m @ w_out
    yl = merge(pl, ya_l, yb_l, yc_l)
    yc = merge(pc, ya_c, yb_c, yc_c) if ctx_out else None
    return yl, yc


def setup_inputs(seed: int = 0) -> dict:
    key = jax.random.key(seed)
    ks = jax.random.split(key, 32)
    def nrm(k, shape, sc):
        return jax.random.normal(k, shape, F32) * sc
    x = nrm(ks[0], (BATCH, SEQ, D_MODEL), 1.0)
    c = nrm(ks[1], (BATCH, D_MODEL), 1.0)
    ctx = nrm(ks[2], (BATCH, CTX_LEN, D_MODEL), 1.0)
    c_ctx = nrm(ks[3], (D_MODEL,), 1.0)
    ada_w = nrm(ks[4], (DEPTH, D_MODEL, N_MOD * D_MODEL), 0.5 * D_MODEL ** -0.5)
    ada_b = nrm(ks[5], (DEPTH, N_MOD * D_MODEL), 0.02)
    norm_g = 1.0 + nrm(ks[6], (DEPTH, 3, D_MODEL), 0.02)
    ffn_w1 = nrm(ks[7], (DEPTH, 2, D_MODEL, D_FF), D_MODEL ** -0.5)
    ffn_w3 = nrm(ks[8], (DEPTH, 2, D_MODEL, D_FF), D_MODEL ** -0.5)
    ffn_w2 = nrm(ks[9], (DEPTH, 2, D_FF, D_MODEL), D_FF ** -0.5)
    w_in = nrm(ks[10], (DEPTH, D_MODEL, D_IN), D_MODEL ** -0.5)
    rg_conv_w = nrm(ks[11], (DEPTH, CONV_W, D_RNN), CONV_W ** -0.5)
    rg_conv_b = nrm(ks[12], (DEPTH, D_RNN), 0.01)
    rg_wa = nrm(ks[13], (DEPTH, 2, RNN_BLOCKS, RNN_BLOCK, RNN_BLOCK), RNN_BLOCK ** -0.5)
    rg_ba = nrm(ks[14], (DEPTH, 2, D_RNN), 0.01)
    rg_wi = nrm(ks[15], (DEPTH, 2, RNN_BLOCKS, RNN_BLOCK, RNN_BLOCK), RNN_BLOCK ** -0.5)
    rg_bi = nrm(ks[16], (DEPTH, 2, D_RNN), 0.01)
    a0 = jax.random.uniform(ks[17], (DEPTH, 2, D_RNN), F32, 0.9, 0.999)
    pa = a0 ** (1.0 / LRU_C)
    rg_lam = jnp.log(pa) - jnp.log1p(-pa)
    ib = nrm(ks[18], (DEPTH, 2, 1, ML_HEADS), 0.1)
    fb = jnp.linspace(3.0, 6.0, ML_HEADS, dtype=F32) + nrm(ks[19], (DEPTH, 2, 1, ML_HEADS), 0.01)
    ml_gate_b = jnp.concatenate([ib, fb], axis=2).reshape(DEPTH, 4 * ML_HEADS)
    ml_norm_g = 1.0 + nrm(ks[20], (DEPTH, ML_W), 0.02)
    at_qn_g = 1.0 + nrm(ks[21], (DEPTH, AT_DH), 0.02)
    at_kn_g = 1.0 + nrm(ks[22], (DEPTH, AT_DH), 0.02)
    at_sink = nrm(ks[23], (DEPTH, AT_HEADS), 0.5)
    w_branch = nrm(ks[24], (DEPTH, N_BRANCH, BRANCH_W, D_MODEL), BRANCH_W ** -0.5)
    w_out = nrm(ks[25], (DEPTH, D_MODEL, D_MODEL), D_MODEL ** -0.5)
    return {'x': x, 'c': c, 'ctx': ctx, 'c_ctx': c_ctx, 'ada_w': ada_w, 'ada_b': ada_b,
            'norm_g': norm_g, 'ffn_w1': ffn_w1, 'ffn_w3': ffn_w3, 'ffn_w2': ffn_w2, 'w_in': w_in,
            'rg_conv_w': rg_conv_w, 'rg_conv_b': rg_conv_b, 'rg_wa': rg_wa, 'rg_ba': rg_ba,
            'rg_wi': rg_wi, 'rg_bi': rg_bi, 'rg_lam': rg_lam, 'ml_gate_b': ml_gate_b,
            'ml_norm_g': ml_norm_g, 'at_qn_g': at_qn_g, 'at_kn_g': at_kn_g, 'at_sink': at_sink,
            'w_branch': w_branch, 'w_out': w_out}


def reference(x, c, ctx, c_ctx, ada_w, ada_b, norm_g, ffn_w1, ffn_w3, ffn_w2, w_in,
              rg_conv_w, rg_conv_b, rg_wa, rg_ba, rg_wi, rg_bi, rg_lam, ml_gate_b,
              ml_norm_g, at_qn_g, at_kn_g, at_sink, w_branch, w_out):
    B, L, _ = x.shape
    cos, sin = axial_angles(L)
    sc = jax.nn.silu(c)
    scc = jax.nn.silu(c_ctx)
    xl, xc = x, ctx
    for l in range(DEPTH):
        ctx_out = l < DEPTH - 1
        ml = (sc @ ada_w[l] + ada_b[l]).reshape(B, 1, N_MOD, D_MODEL)
        mc = (scc @ ada_w[l] + ada_b[l]).reshape(N_MOD, D_MODEL)
        xl = xl + 0.5 * ml[:, :, 2] * swiglu(ln_mod(xl, norm_g[l, 0], ml[:, :, 0], ml[:, :, 1]),
                                             ffn_w1[l, 0], ffn_w3[l, 0], ffn_w2[l, 0])
        xc = xc + 0.5 * mc[2] * swiglu(ln_mod(xc, norm_g[l, 0], mc[0], mc[1]),
                                       ffn_w1[l, 0], ffn_w3[l, 0], ffn_w2[l, 0])
        yl, yc = token_mixer(ln_mod(xl, norm_g[l, 1], ml[:, :, 3], ml[:, :, 4]),
                             ln_mod(xc, norm_g[l, 1], mc[3], mc[4]),
                             w_in[l], rg_conv_w[l], rg_conv_b[l], rg_wa[l], rg_ba[l], rg_wi[l],
                             rg_bi[l], rg_lam[l], ml_gate_b[l], ml_norm_g[l], at_qn_g[l],
                             at_kn_g[l], at_sink[l], w_branch[l], w_out[l], cos, sin, ctx_out)
        xl = xl + ml[:, :, 5] * yl
        xl = xl + 0.5 * ml[:, :, 8] * swiglu(ln_mod(xl, norm_g[l, 2], ml[:, :, 6], ml[:, :, 7]),
                                             ffn_w1[l, 1], ffn_w3[l, 1], ffn_w2[l, 1])
        if ctx_out:
            xc = xc + mc[5] * yc
            xc = xc + 0.5 * mc[8] * swiglu(ln_mod(xc, norm_g[l, 2], mc[6], mc[7]),
                                           ffn_w1[l, 1], ffn_w3[l, 1], ffn_w2[l, 1])
    return xl
```

```python
import contextlib
import math
import numpy as np
import concourse.bass as bass
import concourse.mybir as mybir
from concourse.bass_utils import run_bass_kernel_spmd

F32 = mybir.dt.float32
F32R = mybir.dt.float32r
BF16 = mybir.dt.bfloat16
ALU = mybir.AluOpType
AF = mybir.ActivationFunctionType

D_MODEL = 1024; SEQ = 2048; CTX = 256; NT = SEQ + CTX; DEPTH = 4; D_FF = 2816
NFC = D_FF // 128
D_RNN = 512; ML_W = 512; AT_W = 512; AT_KVW = 128
OFF_RG_X = 0; OFF_RG_G = 512; OFF_ML_Q = 1024; OFF_ML_K = 1536; OFF_ML_V = 2048; OFF_ML_O = 2560
OFF_ML_G = 3072; OFF_AT_Q = 3088; OFF_AT_K = 3600; OFF_AT_V = 3728; OFF_BR_G = 3856
EPS = 1e-6
TT = [(0, 256), (256, 768), (768, 1280), (1280, 1792), (1792, 2304)]
NCH = NT // 128
NSLOT = 7
EMBED_WAIT = True
PIECES_PER_LAYER = 132 + 12 + 17 + 7 + 72
NSP = 160
NROWP = 1040
FFN_GROUPS = [list(range(g, min(g + 4, NFC))) for g in range(0, NFC, 4)]


class Buf:
    __slots__ = ("w", "r")

    def __init__(self):
        self.w = None
        self.r = {}


class Prog:
    ENG = ("pe", "act", "dve", "pool", "sp")

    def __init__(self):
        self.ops = {e: [] for e in self.ENG}
        self.cnt = {e: 0 for e in self.ENG}
        self.seen = {e: {} for e in self.ENG}
        self.dcnt = {}

    def _waits(self, eng, r, w):
        deps = {}

        def add(tok):
            if tok is None:
                return
            k, v = tok
            if deps.get(k, 0) < v:
                deps[k] = v
        for b in r:
            add(b.w)
        for b in w:
            add(b.w)
            for k, v in b.r.items():
                add((k, v))
        out = []
        for k, v in deps.items():
            if k == eng and eng == "pe":
                continue
            if self.seen[eng].get(k, 0) < v:
                self.seen[eng][k] = v
                out.append((k, v))
        return out

    def op(self, eng, fn, r=(), w=()):
        waits = self._waits(eng, r, w)
        self.cnt[eng] += 1
        n = self.cnt[eng]
        self.ops[eng].append((waits, fn, (eng, 1)))
        for b in r:
            if b.r.get(eng, 0) < n:
                b.r[eng] = n
        for b in w:
            b.w = (eng, n)
            b.r = {}

    def dma(self, q, fn, semkey, r=(), w=()):
        waits = self._waits(q, r, w)
        self.dcnt[semkey] = self.dcnt.get(semkey, 0) + 16
        n = self.dcnt[semkey]
        self.ops[q].append((waits, fn, (semkey, 16)))
        for b in r:
            b.r[semkey] = n
        for b in w:
            b.w = (semkey, n)
            b.r = {}

    def fence(self):
        for e in ("pe", "act", "dve"):
            waits = []
            for o in ("pe", "act", "dve"):
                if o == e and e == "pe":
                    continue
                if self.seen[e].get(o, 0) < self.cnt[o]:
                    self.seen[e][o] = self.cnt[o]
                    waits.append((o, self.cnt[o]))
            if waits:
                self.ops[e].append((waits, None, None))

    def wait_only(self, eng, toks):
        waits = []
        for k, v in toks:
            if self.seen[eng].get(k, 0) < v:
                self.seen[eng][k] = v
                waits.append((k, v))
        if waits:
            self.ops[eng].append((waits, None, None))


def fm_piece(W, c0, ncols=128):
    blk = np.zeros((1024, 128), np.float32)
    blk[:, :ncols] = W[:, c0:c0 + ncols]
    return np.ascontiguousarray(blk.reshape(8, 128, 128).transpose(1, 0, 2)).reshape(128, 1024)


def fm_piece_cols(cols):
    blk = np.zeros((1024, 128), np.float32)
    blk[:, :cols.shape[1]] = cols
    return np.ascontiguousarray(blk.reshape(8, 128, 128).transpose(1, 0, 2)).reshape(128, 1024)


def fm_vec(v):
    return np.ascontiguousarray(v.reshape(-1, 128).T)


def make_consts():
    c = {}
    c["ident"] = np.eye(128, dtype=np.float32)
    c["ones"] = np.ones((128, 128), np.float32)
    bo = np.zeros((128, 128), np.float32)
    bo[:64, :64] = 1.0 / 64
    bo[64:, 64:] = 1.0 / 64
    c["bones"] = bo
    s = np.arange(128)[:, None]
    t = np.arange(128)[None, :]
    c["tri_le"] = (s <= t).astype(np.float32)
    c["tri_ge"] = (s >= t).astype(np.float32)
    R = np.zeros((64, 64), np.float32)
    for m in range(16):
        R[m, m + 16] = -1.0
        R[m + 16, m] = 1.0
        R[m + 32, m + 48] = -1.0
        R[m + 48, m + 32] = 1.0
    RT = np.zeros((128, 128), np.float32)
    RT[:64, :64] = R.T
    RT[64:, 64:] = R.T
    c["rt"] = RT
    L = SEQ
    rows = L // 64
    row = np.repeat(np.arange(rows), 64).astype(np.float32)
    col = np.tile(np.arange(64), rows).astype(np.float32)
    half = 32
    inv = (10000.0 ** (-np.arange(0, half, 2, dtype=np.float32) / half)).astype(np.float32)
    ar = row[:, None] * inv
    ac = col[:, None] * inv
    ang = np.concatenate([ar, ar, ac, ac], axis=-1).astype(np.float32)
    c["cos"] = np.ascontiguousarray(np.concatenate([np.cos(ang).T, np.cos(ang).T], 0)).astype(np.float32)
    c["sin"] = np.ascontiguousarray(np.concatenate([np.sin(ang).T, np.sin(ang).T], 0)).astype(np.float32)
    return c


CONST_SQ = ["ident", "ones", "bones", "tri_le", "tri_ge", "rt"]


def build_host_streams(inp, NL):
    pieces = []
    ada = []
    sp = np.zeros((NL, 128, NSP), np.float32)
    rowp = np.zeros((NL, 128, NROWP), np.float32)
    for l in range(NL):
        w_in = inp["w_in"][l]
        for j in range(72):
            ada.append(fm_piece(inp["ada_w"][l], j * 128))

        def ffn_pieces(i, ada_next=None):
            for gi, grp in enumerate(FFN_GROUPS):
                for f in grp:
                    pieces.append(fm_piece(inp["ffn_w1"][l, i], f * 128))
                    pieces.append(fm_piece(inp["ffn_w3"][l, i], f * 128))
                for f in grp:
                    pieces.append(np.ascontiguousarray(inp["ffn_w2"][l, i][f * 128:(f + 1) * 128, :]))

        def fold_pieces(j):
            for o in range(8):
                pieces.append(fm_piece(w_in, OFF_BR_G + j * 1024 + o * 128))
                wb = inp["w_branch"][l, j][:, o * 128:(o + 1) * 128]
                blk = np.zeros((128, 1024), np.float32)
                blk[:, :512] = wb.reshape(4, 128, 128).transpose(1, 0, 2).reshape(128, 512)
                pieces.append(blk)
            for o2 in range(8):
                pieces.append(fm_piece(inp["w_out"][l], o2 * 128))

        ffn_pieces(0)
        for c in range(4):
            pieces.append(fm_piece(w_in, OFF_RG_X + c * 128))
            pieces.append(fm_piece(w_in, OFF_RG_G + c * 128))
            blk = np.zeros((128, 1024), np.float32)
            for idx, (arr, d) in enumerate([(inp["rg_wa"], 0), (inp["rg_wi"], 0), (inp["rg_wa"], 1), (inp["rg_wi"], 1)]):
                blk[0:64, idx * 128:idx * 128 + 64] = arr[l, d, 2 * c]
                blk[64:128, idx * 128 + 64:idx * 128 + 128] = arr[l, d, 2 * c + 1]
            pieces.append(blk)
        fold_pieces(0)
        gc = np.zeros((1024, 16), np.float32)
        for d in range(2):
            for h in range(4):
                gc[:, d * 4 + h] = w_in[:, OFF_ML_G + (2 * d) * 4 + h]
                gc[:, 8 + d * 4 + h] = w_in[:, OFF_ML_G + (2 * d + 1) * 4 + h]
        pieces.append(fm_piece_cols(gc))
        for h in range(4):
            for off in (OFF_ML_Q, OFF_ML_K, OFF_ML_V, OFF_ML_O):
                pieces.append(fm_piece(w_in, off + h * 128))
        fold_pieces(1)
        pieces.append(fm_piece(w_in, OFF_AT_V))
        for g in range(2):
            kc_ = w_in[:, OFF_AT_K + g * 64:OFF_AT_K + (g + 1) * 64]
            pieces.append(fm_piece_cols(np.concatenate([kc_, kc_], 1)))
            for qc in range(2):
                pieces.append(fm_piece(w_in, OFF_AT_Q + (2 * g + qc) * 128))
        fold_pieces(2)
        ffn_pieces(1, ada_next=(l + 1 if l + 1 < NL else None))
        o = 0
        for i in range(3):
            sp[l, :, o:o + 8] = fm_vec(inp["norm_g"][l, i]); o += 8
        sp[l, :, o:o + 72] = fm_vec(inp["ada_b"][l]); o += 72
        for k in range(4):
            sp[l, :, o:o + 4] = fm_vec(inp["rg_conv_w"][l, k]); o += 4
        sp[l, :, o:o + 4] = fm_vec(inp["rg_conv_b"][l]); o += 4
        for nm in ("rg_ba", "rg_bi", "rg_lam"):
            for d in range(2):
                sp[l, :, o:o + 4] = fm_vec(inp[nm][l, d]); o += 4
        sp[l, :, o] = np.tile(inp["at_qn_g"][l], 2); o += 1
        sp[l, :, o] = np.tile(inp["at_kn_g"][l], 2); o += 1
        gb = inp["ml_gate_b"][l]
        g16 = np.zeros(16, np.float32)
        for d in range(2):
            for h in range(4):
                g16[d * 4 + h] = gb[(2 * d) * 4 + h]
                g16[8 + d * 4 + h] = gb[(2 * d + 1) * 4 + h]
        rowp[l, :, 0:16] = g16[None, :]
        rowp[l, :, 16:528] = inp["ml_norm_g"][l][None, :]
        for c in range(4):
            for p in range(128):
                pass
        sk = inp["at_sink"][l]
        rowp[l, :, 528:1040] = np.repeat(sk, 64)[None, :]
    return (np.stack(pieces), np.stack(ada), sp, rowp)


def build_program(NL, dbg_skip=(), dbg_snap=False):
    nc = bass.Bass("TRN2", target_bir_lowering=False)
    P = Prog()
    NP = NL * PIECES_PER_LAYER
    dr = {}
    dr["xT"] = nc.dram_tensor("xT", [128, 8, NT], F32, kind="ExternalInput").ap()
    dr["scin"] = nc.dram_tensor("scin", [128, 16], F32, kind="ExternalInput").ap()
    dr["ws"] = nc.dram_tensor("ws", [NP, 128, 1024], F32, kind="ExternalInput").ap()
    dr["adas"] = nc.dram_tensor("adas", [NL * 72, 128, 1024], F32, kind="ExternalInput").ap()
    dr["sp"] = nc.dram_tensor("sp", [NL, 128, NSP], F32, kind="ExternalInput").ap()
    dr["rowp"] = nc.dram_tensor("rowp", [NL, 128, NROWP], F32, kind="ExternalInput").ap()
    dr["csq"] = nc.dram_tensor("csq", [128, len(CONST_SQ) * 128], F32, kind="ExternalInput").ap()
    dr["cossin"] = nc.dram_tensor("cossin", [128, 2 * SEQ], BF16, kind="ExternalInput").ap()
    dr["yT"] = nc.dram_tensor("yT", [128, 8, SEQ], F32, kind="ExternalOutput").ap()
    if dbg_snap:
        dr["dbg"] = nc.dram_tensor("dbg", [4, 128, 8, SEQ], F32, kind="ExternalOutput").ap()

    es = contextlib.ExitStack()
    with es:
        def sb(name, shape, dt):
            return es.enter_context(nc.sbuf_tensor(name, shape, dt))
        X = sb("X", [128, 8, NT], F32)
        H = sb("H", [128, 8, NT], BF16)
        WR = sb("WR", [128, NSLOT, 1024], BF16)
        ARENA_W = 18600
        ARENA = sb("ARENA", [128, ARENA_W], F32)
        CSQ = sb("CSQ", [128, len(CONST_SQ) * 128], F32)
        CSQB = sb("CSQB", [128, len(CONST_SQ) * 128], BF16)
        SPT2 = [sb("SPT0", [128, NSP], F32), sb("SPT1", [128, NSP], F32)]
        HB2 = [sb("HBA0", [128, 16], F32), sb("HBA1", [128, 16], F32)]
        ROWP = sb("ROWP", [128, NROWP], F32)
        SINKB = sb("SINKB", [1, 512], BF16)
        MODS2 = [sb("MODS0", [128, 72, 2], F32), sb("MODS1", [128, 72, 2], F32)]
        AM2 = [sb("AM0", [128, 3, 8, 2], F32), sb("AM1", [128, 3, 8, 2], F32)]
        GH2 = [sb("GH0", [128, 3, 8, 2], F32), sb("GH1", [128, 3, 8, 2], F32)]
        SC = sb("SC", [128, 8, 2], F32)
        SCB = sb("SCB", [128, 8, 2], BF16)
        SMALL = sb("SMALL", [128, 64], F32)
        PS = [es.enter_context(nc.psum_tensor("ps%d" % i, [128, 512], F32)) for i in range(8)]
        PB = [Buf() for _ in range(8)]
        psi = [0]

        def next_ps():
            i = psi[0] % 8
            psi[0] += 1
            return PS[i], PB[i]

        semkeys = ["pe", "act", "dve", "pool", "xld", "cst", "cs2", "spl0", "spl1", "rwp", "out"] + [("w", s) for s in range(NSLOT)] + [("a", s) for s in range(12)]
        sems = {}
        for k in semkeys:
            nm = k if isinstance(k, str) else "%s%d" % k
            sems[k] = es.enter_context(nc.semaphore("s_" + nm))

        XB = [[Buf() for _ in TT] for _ in range(8)]
        HB = [[Buf() for _ in TT] for _ in range(8)]
        WB = [Buf() for _ in range(NSLOT)]
        b_csq, b_csqb, b_sp, b_rowp, b_sink, b_sinkb, b_mods, b_am, b_gh, b_sc, b_small = (Buf() for _ in range(11))
        b_sp2 = [Buf(), Buf()]; b_mods2 = [Buf(), Buf()]; b_am2 = [Buf(), Buf()]; b_gh2 = [Buf(), Buf()]
        class NS:
            pass
        sets = []
        for par in range(2):
            ns = NS()
            ns.HB_ = HB2[par]
            ns.SPT = SPT2[par]; ns.MODS = MODS2[par]; ns.AM = AM2[par]; ns.GH = GH2[par]
            ns.b_sp = b_sp2[par]; ns.b_mods = b_mods2[par]; ns.b_am = b_am2[par]; ns.b_gh = b_gh2[par]
            sets.append(ns)
        CUR = [sets[0]]
        b_cs = Buf()

        def cq(name, bf=False):
            i = CONST_SQ.index(name)
            return (CSQB if bf else CSQ)[:, i * 128:(i + 1) * 128]

        def mm(out, lhsT, rhs, start, stop, r, w):
            P.op("pe", lambda e: e.matmul(out, lhsT, rhs, start=start, stop=stop), r, w)

        def tr(out, in_, ident, r, w):
            P.op("pe", lambda e: e.transpose(out, in_, ident), r, w)

        def act(out, in_, func, r, w, bias=None, scale=None):
            kw = {}
            if bias is not None:
                kw["bias"] = bias
            if scale is not None:
                kw["scale"] = scale
            P.op("act", lambda e: e.activation(out, in_, func, **kw), r, w)

        def tt(out, a, b, op, r, w, eng="dve"):
            eng = "dve"
            P.op(eng, lambda e: e.tensor_tensor(out, a, b, op), r, w)

        def ts(out, a, s1, s2, op0, op1, r, w, eng="dve"):
            eng = "dve"
            if s2 is None:
                P.op(eng, lambda e: e.tensor_scalar(out, a, s1, None, op0), r, w)
            else:
                P.op(eng, lambda e: e.tensor_scalar(out, a, s1, s2, op0, op1), r, w)

        def stt(out, a, s, b, op0, op1, r, w):
            P.op("dve", lambda e: e.scalar_tensor_tensor(out, a, s, b, op0, op1), r, w)

        def recip(out, a, r, w):
            P.op("dve", lambda e: e.reciprocal(out, a), r, w)

        def scan(out, d0, d1, init, r, w):
            P.op("dve", lambda e: e.tensor_tensor_scan(out, d0, d1, init, ALU.mult, ALU.add), r, w)

        def cp(out, in_, r, w, eng="act"):
            if eng == "act":
                act(out, in_, AF.Copy, r, w)
            else:
                P.op("dve", lambda e: e.tensor_copy(out, in_), r, w)

        class Arena:
            def __init__(self):
                self.off = 0

            def reset(self):
                P.fence()
                self.off = 0

            def f32(self, n):
                a = ARENA[:, self.off:self.off + n]
                self.off += n
                assert self.off <= ARENA_W, self.off
                return a

            def bf(self, n):
                assert n % 2 == 0
                a = ARENA[:, self.off:self.off + n // 2].bitcast(BF16)
                self.off += n // 2
                assert self.off <= ARENA_W, self.off
                return a
        AR = Arena()

        pc = [0]
        sl = [0]

        def wpiece():
            i = pc[0]
            pc[0] += 1
            s = sl[0] % NSLOT
            sl[0] += 1
            P.dma("pool", lambda e: e.dma_start(out=WR[:, s, :], in_=dr["ws"][i]), ("w", s), w=[WB[s]])
            return s

        for kc in range(8):
            P.dma("sp", (lambda kc: lambda e: e.dma_start(out=X[:, kc, :], in_=dr["xT"][:, kc, :]))(kc), "xld", w=XB[kc])
        for kc in range(8):
            for b in XB[kc]:
                b.w = ("xld", P.dcnt["xld"])
        P.dma("sp", lambda e: e.dma_start(out=CSQ[:, :], in_=dr["csq"]), "cst", w=[b_csq])
        P.dma("sp", lambda e: e.dma_start(out=SC[:, :, :].rearrange("p a b -> p (a b)"), in_=dr["scin"]), "cst", w=[b_sc])
        b_csq.w = ("cst", P.dcnt["cst"])
        b_sc.w = ("cst", P.dcnt["cst"])
        cp(CSQB[:, :], CSQ[:, :], [b_csq], [b_csqb], eng="dve")
        P.op("dve", lambda e: e.memset(SMALL[:, 0:1], EPS), (), [b_small])
        P.op("dve", lambda e: e.memset(SMALL[:, 1:2], 1.0), (), [b_small])
        P.op("dve", lambda e: e.memset(SMALL[:, 2:3], math.log(128.0 ** -0.5)), (), [b_small])
        P.op("dve", lambda e: e.memset(SMALL[:, 3:4], 0.25), (), [b_small])
        EPSC = SMALL[:, 0:1]
        ONEC = SMALL[:, 1:2]
        LNSC = SMALL[:, 2:3]
        QUARTC = SMALL[:, 3:4]
        b_scb = Buf()
        act(SCB[:, :, :], SC[:, :, :], AF.Silu, [b_sc], [b_scb])

        def load_spt(l, T):
            key = "spl%d" % (l % 2)
            P.dma("sp", lambda e: e.dma_start(out=T.SPT[:, :], in_=dr["sp"][l]), key, w=[T.b_sp])

        def load_rowp(l):
            P.dma("sp", lambda e: e.dma_start(out=ROWP[:, :], in_=dr["rowp"][l]), "rwp", w=[b_rowp])
            act(SINKB[:, :], ROWP[0:1, 528:1040], AF.Exp, [b_rowp], [b_sinkb])
        NG = lambda i: CUR[0].SPT[:, i * 8:(i + 1) * 8]
        CW = lambda k, c: CUR[0].SPT[:, 96 + k * 4 + c:97 + k * 4 + c]
        CBv = lambda c: CUR[0].SPT[:, 112 + c:113 + c]
        BA = lambda d, c: CUR[0].SPT[:, 116 + d * 4 + c:117 + d * 4 + c]
        BI = lambda d, c: CUR[0].SPT[:, 124 + d * 4 + c:125 + d * 4 + c]
        QNG = lambda: CUR[0].SPT[:, 140:141]
        KNG = lambda: CUR[0].SPT[:, 141:142]
        COEFf = lambda: CUR[0].SPT[:, 144:152]
        COEFH = lambda: CUR[0].SPT[:, 152:160]
        BAH = lambda d, c: CUR[0].HB_[:, d * 4 + c:d * 4 + c + 1]
        BIH = lambda d, c: CUR[0].HB_[:, 8 + d * 4 + c:8 + d * 4 + c + 1]

        NAS = 12
        ADA_OFF = 7000
        ada_state = {"q": 0, "bufs": None, "idx": 0}

        def ada_slots_reset():
            P.wait_only("pool", [(e_, P.cnt[e_]) for e_ in ("pe", "act", "dve")])
            ada_state["bufs"] = [Buf() for _ in range(NAS)]
            ada_state["q"] = 0

        def ada_part(T, part):
            ps, pb = next_ps()
            j0 = part * 12
            for jj in range(12):
                k = ada_state["q"] % NAS
                ada_state["q"] += 1
                i = ada_state["idx"]
                ada_state["idx"] += 1
                slot = ARENA[:, ADA_OFF + k * 512:ADA_OFF + (k + 1) * 512].bitcast(BF16)
                sb_ = ada_state["bufs"][k]
                P.dma("pool", (lambda slot, i: lambda e: e.dma_start(out=slot, in_=dr["adas"][i]))(slot, i), ("a", k), w=[sb_])
                for kc in range(8):
                    mm(ps[:, 2 * jj:2 * jj + 2], slot[:, kc * 128:(kc + 1) * 128], SCB[:, kc, :],
                       kc == 0, kc == 7, [sb_, b_scb], [pb])
            psv = ps[:, 0:24].rearrange("p (j n) -> p j n", n=2)
            for n in range(2):
                tt(T.MODS[:, j0:j0 + 12, n], psv[:, :, n], T.SPT[:, 24 + j0:24 + j0 + 12], ALU.add, [pb, T.b_sp], [T.b_mods])

        def ada_finish(T):
            for i in range(3):
                for n in range(2):
                    stt(T.AM[:, i, :, n], T.MODS[:, (3 * i + 1) * 8:(3 * i + 2) * 8, n], 1.0, T.SPT[:, i * 8:(i + 1) * 8],
                        ALU.add, ALU.mult, [T.b_mods, T.b_sp], [T.b_am])
                    ts(T.GH[:, i, :, n], T.MODS[:, (3 * i + 2) * 8:(3 * i + 3) * 8, n], 0.5 if i != 1 else 1.0, None, ALU.mult, None,
                       [T.b_mods], [T.b_gh])
            COEF = T.SPT[:, 144:152]
            act(COEF, T.SPT[:, 132:140], AF.Exp, [T.b_sp], [T.b_sp], scale=-1.0)
            act(COEF, COEF, AF.Ln, [T.b_sp], [T.b_sp], bias=ONEC)
            ts(COEF, COEF, -8.0, None, ALU.mult, None, [T.b_sp], [T.b_sp])
            ts(T.SPT[:, 152:160], COEF, 0.5, None, ALU.mult, None, [T.b_sp], [T.b_sp])
            ts(T.HB_[:, 0:16], T.SPT[:, 116:132], 0.5, None, ALU.mult, None, [T.b_sp], [T.b_sp])

        def shift_ap(i, kc, n):
            return CUR[0].MODS[:, (3 * i) * 8 + kc, n:n + 1]

        NORM_OFF = ARENA_W - 3072
        nrm = {"SQb": [Buf(), Buf()], "RSb": [Buf(), Buf()], "TMPb": [Buf(), Buf()]}

        def norm(i):
            SQ = [ARENA[:, NORM_OFF + k * 512:NORM_OFF + (k + 1) * 512] for k in range(2)]
            SQb = nrm["SQb"]
            RS = [ARENA[:, NORM_OFF + (2 + k) * 512:NORM_OFF + (3 + k) * 512] for k in range(2)]
            RSb = nrm["RSb"]
            TMP = [ARENA[:, NORM_OFF + (4 + k) * 512:NORM_OFF + (5 + k) * 512] for k in range(2)]
            TMPb = nrm["TMPb"]
            st = {"q": 0}

            def stats(t):
                a, b = TT[t]
                W = b - a
                ps, pb = next_ps()
                for kc in range(8):
                    j = st["q"] % 2
                    st["q"] += 1
                    sqb = SQ[j][:, 0:256].bitcast(BF16)
                    if kc % 2 == 0:
                        act(sqb[:, :W], X[:, kc, a:b], AF.Square, [XB[kc][t]], [SQb[j]])
                    else:
                        tt(sqb[:, :W], X[:, kc, a:b], X[:, kc, a:b], ALU.mult, [XB[kc][t]], [SQb[j]])
                    mm(ps[:, :W], cq("ones", True), sqb[:, :W], kc == 0, kc == 7, [SQb[j], b_csqb], [pb])
                j = t % 2
                act(RS[j][:, :W], ps[:, :W], AF.Ln, [pb, b_small], [RSb[j]], bias=EPSC, scale=1.0 / D_MODEL)
                act(RS[j][:, :W], RS[j][:, :W], AF.Exp, [RSb[j]], [RSb[j]], scale=-0.5)

            def modulate(t):
                a, b = TT[t]
                W = b - a
                n = 1 if t == 0 else 0
                j = t % 2
                for kc in range(8):
                    jj = st["q"] % 2
                    st["q"] += 1
                    stt(TMP[jj][:, :W], X[:, kc, a:b], CUR[0].AM[:, i, kc, n:n + 1], RS[j][:, :W], ALU.mult, ALU.mult,
                        [XB[kc][t], CUR[0].b_am, RSb[j]], [TMPb[jj]])
                    act(H[:, kc, a:b], TMP[jj][:, :W], AF.Identity, [TMPb[jj], CUR[0].b_mods], [HB[kc][t]],
                        bias=shift_ap(i, kc, n), scale=1.0)
            stats(0)
            for t in range(len(TT)):
                if t + 1 < len(TT):
                    stats(t + 1)
                modulate(t)

        def Hall(t):
            return [HB[kc][t] for kc in range(8)]

        def proj_fm(s, t, ps, pb, M=128):
            a, b = TT[t]
            for kc in range(8):
                mm(ps[:M, :b - a], WR[:, s, kc * 128:kc * 128 + M], H[:, kc, a:b], kc == 0, kc == 7,
                   [WB[s], HB[kc][t]], [pb])

        def tile_of_chunk(ch):
            tok = ch * 128
            for t, (a, b) in enumerate(TT):
                if a <= tok < b:
                    return t

        def proj_tm(s, ch, ps, pb, ncols, col0=0):
            t = tile_of_chunk(ch)
            for kc in range(8):
                mm(ps[:, col0:col0 + ncols], H[:, kc, ch * 128:(ch + 1) * 128], WR[:, s, kc * 128:kc * 128 + ncols],
                   kc == 0, kc == 7, [WB[s], HB[kc][t]], [pb])

        ffn_b = {"Gb": [[Buf() for _ in TT] for _ in range(4)], "SILb": [Buf(), Buf()]}

        def ffn(i, extra=None, pre=None, fence=True):
            if fence:
                AR.reset()
            else:
                AR.off = 0
            if pre is not None:
                pre()
            G = AR.bf(4 * NT)
            Gb = ffn_b["Gb"]
            SIL = [AR.f32(512), AR.f32(512)]
            SILb = ffn_b["SILb"]
            q = 0
            for grp in FFN_GROUPS:
                for fi, f in enumerate(grp):
                    s1 = wpiece()
                    s3 = wpiece()
                    for t, (a, b) in enumerate(TT):
                        W = b - a
                        pa, pab = next_ps()
                        proj_fm(s1, t, pa, pab)
                        pb_, pbb = next_ps()
                        proj_fm(s3, t, pb_, pbb)
                        j = q % 2
                        q += 1
                        act(SIL[j][:, :W], pa[:, :W], AF.Silu, [pab], [SILb[j]])
                        tt(G[:, fi * NT + a:fi * NT + b], SIL[j][:, :W], pb_[:, :W], ALU.mult, [SILb[j], pbb], [Gb[fi][t]])
                s2 = [wpiece() for _ in grp]
                for t, (a, b) in enumerate(TT):
                    n = 1 if t == 0 else 0
                    W = b - a
                    for o in range(8):
                        ps, pb = next_ps()
                        for fi, f in enumerate(grp):
                            mm(ps[:, :W], WR[:, s2[fi], o * 128:(o + 1) * 128], G[:, fi * NT + a:fi * NT + b],
                               fi == 0, fi == len(grp) - 1, [WB[s2[fi]], Gb[fi][t]], [pb])
                        stt(X[:, o, a:b], ps[:, :W], CUR[0].GH[:, i, o, n:n + 1], X[:, o, a:b], ALU.mult, ALU.add,
                            [pb, CUR[0].b_gh, XB[o][t]], [XB[o][t]])
                if extra is not None:
                    extra(FFN_GROUPS.index(grp))

        def fold(Y, Yb):
            P.fence()
            AR.off = 4 * NT // 2
            MJ = AR.bf(8 * NT)
            MJb = [[Buf() for _ in TT] for _ in range(8)]
            SG = [AR.f32(512), AR.f32(512)]
            SGb = [Buf(), Buf()]
            q = 0
            for o in range(8):
                sg = wpiece()
                sbp = wpiece()
                for t, (a, b) in enumerate(TT):
                    W = b - a
                    pg, pgb = next_ps()
                    proj_fm(sg, t, pg, pgb)
                    pbr, pbrb = next_ps()
                    for kc in range(4):
                        mm(pbr[:, :W], WR[:, sbp, kc * 128:(kc + 1) * 128], Y[:, kc * NT + a:kc * NT + b], kc == 0, kc == 3,
                           [WB[sbp], Yb[kc][t]], [pbrb])
                    j = q % 2
                    q += 1
                    act(SG[j][:, :W], pg[:, :W], AF.Sigmoid, [pgb], [SGb[j]])
                    tt(MJ[:, o * NT + a:o * NT + b], SG[j][:, :W], pbr[:, :W], ALU.mult, [SGb[j], pbrb], [MJb[o][t]])
            for o2 in range(8):
                so = wpiece()
                for t, (a, b) in enumerate(TT):
                    n = 1 if t == 0 else 0
                    W = b - a
                    ps, pb = next_ps()
                    for o in range(8):
                        mm(ps[:, :W], WR[:, so, o * 128:(o + 1) * 128], MJ[:, o * NT + a:o * NT + b], o == 0, o == 7,
                           [WB[so], MJb[o][t]], [pb])
                    stt(X[:, o2, a:b], ps[:, :W], CUR[0].GH[:, 1, o2, n:n + 1], X[:, o2, a:b], ALU.mult, ALU.add,
                        [pb, CUR[0].b_gh, XB[o2][t]], [XB[o2][t]])

        def new_Y():
            Y = AR.bf(4 * NT)
            Yb = [[Buf() for _ in TT] for _ in range(4)]
            return Y, Yb

        def rglru():
            AR.reset()
            Y, Yb = new_Y()
            T1 = AR.f32(NT); T2 = AR.f32(NT); T3 = AR.f32(NT); T4 = AR.f32(NT); T5 = AR.f32(NT)
            UB = AR.bf(NT)
            nt_ = len(TT)
            b1 = [Buf() for _ in TT]; b2 = [Buf() for _ in TT]; b3 = [Buf() for _ in TT]
            b4 = [Buf() for _ in TT]; b5 = [Buf() for _ in TT]; bub = [Buf() for _ in TT]
            SEG = {0: (0, 256)}
            for t in range(1, nt_):
                SEG[t] = (256, NT)
            for c in range(4):
                sx = wpiece(); sgp = wpiece(); sw = wpiece()
                for t, (a, b) in enumerate(TT):
                    ps, pb = next_ps()
                    proj_fm(sx, t, ps, pb)
                    cp(T1[:, a:b], ps[:, :b - a], [pb], [b1[t]])
                for t, (a, b) in enumerate(TT):
                    s0, s1 = SEG[t]
                    nb_ = [b1[t]] + ([b1[t - 1]] if t - 1 >= 0 and SEG[t - 1] == SEG[t] else []) + \
                          ([b1[t + 1]] if t + 1 < nt_ and SEG[t + 1] == SEG[t] else [])
                    ts(T2[:, a:b], T1[:, a:b], CW(2, c), CBv(c), ALU.mult, ALU.add, [b1[t], CUR[0].b_sp], [b2[t]])
                    for k, off in ((0, -2), (1, -1), (3, 1)):
                        da, db = max(a, s0 - off), min(b, s1 - off)
                        stt(T2[:, da:db], T1[:, da + off:db + off], CW(k, c), T2[:, da:db], ALU.mult, ALU.add,
                            nb_ + [CUR[0].b_sp, b2[t]], [b2[t]])
                    cp(UB[:, a:b], T2[:, a:b], [b2[t]], [bub[t]])
                for d in range(2):
                    A_ = T1; I_ = T3; S_ = T4
                    cf = COEFH()[:, d * 4 + c:d * 4 + c + 1]
                    for t, (a, b) in enumerate(TT):
                        W = b - a
                        ps, pb = next_ps()
                        mm(ps[:, :W], WR[:, sw, (2 * d) * 128:(2 * d + 1) * 128], UB[:, a:b], True, True, [WB[sw], bub[t]], [pb])
                        act(A_[:, a:b], ps[:, :W], AF.Tanh, [pb, CUR[0].b_sp], [b1[t]], bias=BAH(d, c), scale=0.5)
                        act(A_[:, a:b], A_[:, a:b], AF.Exp, [b1[t], CUR[0].b_sp], [b1[t]], scale=cf, bias=cf)
                        ps2, pb2 = next_ps()
                        mm(ps2[:, :W], WR[:, sw, (2 * d + 1) * 128:(2 * d + 2) * 128], UB[:, a:b], True, True, [WB[sw], bub[t]], [pb2])
                        act(I_[:, a:b], ps2[:, :W], AF.Tanh, [pb2, CUR[0].b_sp], [b3[t]], bias=BIH(d, c), scale=0.5)
                        stt(I_[:, a:b], I_[:, a:b], 1.0, T2[:, a:b], ALU.add, ALU.mult, [b3[t], b2[t]], [b3[t]])
                        act(S_[:, a:b], A_[:, a:b], AF.Square, [b1[t]], [b4[t]])
                    act(S_[:, :], S_[:, :], AF.Sqrt, b4 + [b_small], b4, bias=QUARTC, scale=-0.25)
                    for t, (a, b) in enumerate(TT):
                        tt(I_[:, a:b], I_[:, a:b], S_[:, a:b], ALU.mult, [b3[t], b4[t]], [b3[t]])
                    if d == 0:
                        for t, (a, b) in enumerate(TT):
                            init = 0.0 if t == 0 else T5[:, a - 1:a]
                            rr = [b1[t], b3[t]] + ([b5[t - 1]] if t > 0 else [])
                            scan(T5[:, a:b], A_[:, a:b], I_[:, a:b], init, rr, [b5[t]])
                    else:
                        scan(T4[:, 255::-1], A_[:, 255::-1], I_[:, 255::-1], 0.0, [b1[0], b3[0], b4[0]], [b4[0]])
                        prev_t = 0
                        prev_col = 0
                        for t in range(nt_ - 1, 0, -1):
                            a, b = TT[t]
                            scan(T4[:, b - 1:a - 1:-1], A_[:, b - 1:a - 1:-1], I_[:, b - 1:a - 1:-1], T4[:, prev_col:prev_col + 1],
                                 [b1[t], b3[t], b4[t], b4[prev_t]], [b4[t]])
                            prev_t = t
                            prev_col = a
                for t, (a, b) in enumerate(TT):
                    tt(T5[:, a:b], T5[:, a:b], T4[:, a:b], ALU.add, [b5[t], b4[t]], [b5[t]])
                for t, (a, b) in enumerate(TT):
                    W = b - a
                    ps, pb = next_ps()
                    proj_fm(sgp, t, ps, pb)
                    g_ = T1[:, a:b]
                    act(g_, ps[:, :W], AF.Square, [pb], [b1[t]])
                    ts(g_, g_, 0.044715, 1.0, ALU.mult, ALU.add, [b1[t]], [b1[t]])
                    tt(g_, g_, ps[:, :W], ALU.mult, [b1[t], pb], [b1[t]])
                    act(g_, g_, AF.Tanh, [b1[t]], [b1[t]], scale=0.7978845608028654)
                    stt(g_, g_, 1.0, ps[:, :W], ALU.add, ALU.mult, [b1[t], pb], [b1[t]])
                    stt(Y[:, c * NT + a:c * NT + b], g_, 0.5, T5[:, a:b], ALU.mult, ALU.mult, [b1[t], b5[t]], [Yb[c][t]])
            fold(Y, Yb)

        def mlstm():
            import os
            STOP = float(os.environ.get("ML_STOP", "9"))
            pc_start = pc[0]

            def bail():
                pc[0] = pc_start + 41
            AR.reset()
            Y, Yb = new_Y()
            GTM = AR.f32(NCH * 16)
            gv = GTM.rearrange("p (c j) -> p c j", j=16)
            bg = Buf()
            sgate = wpiece()
            for ch in range(NCH):
                ps, pb = next_ps()
                proj_tm(sgate, ch, ps, pb, 16)
                tt(gv[:, ch, :], ps[:, 0:16], ROWP[:, 0:16], ALU.add, [pb, b_rowp], [bg])
            if STOP <= 1:
                return bail()
            LF = AR.f32(NCH * 8); lfv = LF.rearrange("p (j c) -> p c j", j=8)
            Bm = AR.f32(NCH * 8); bv = Bm.rearrange("p (j c) -> p c j", j=8)
            TOT = AR.f32(NCH * 8); totv = TOT.rearrange("p (j c) -> p c j", j=8)
            EC1 = AR.f32(NCH * 8); ec1v = EC1.rearrange("p (j c) -> p c j", j=8)
            ENB = AR.f32(NCH * 8); enbv = ENB.rearrange("p (j c) -> p c j", j=8)
            WWt = AR.f32(NCH * 8); wwv = WWt.rearrange("p (j c) -> p c j", j=8)
            DEC = AR.f32(NCH * 8); decv = DEC.rearrange("p (j c) -> p c j", j=8)
            bl = Buf()
            act(lfv, gv[:, :, 8:16], AF.Exp, [bg], [bl], scale=-1.0)
            act(lfv, lfv, AF.Ln, [bl, b_small], [bl], bias=ONEC)
            ts(LF, LF, -1.0, None, ALU.mult, None, [bl], [bl])
            ps, pb = next_ps()
            mm(ps[:, 0:4 * NCH], cq("tri_le"), LF[:, 0:4 * NCH], True, True, [bl, b_csq], [pb])
            mm(ps[:, 4 * NCH:8 * NCH], cq("tri_ge"), LF[:, 4 * NCH:8 * NCH], True, True, [bl, b_csq], [pb])
            ps2, pb2 = next_ps()
            mm(ps2[:, 0:NCH * 8], cq("ones"), LF, True, True, [bl, b_csq], [pb2])
            bb_ = Buf()
            cp(Bm, ps[:, 0:NCH * 8], [pb], [bb_], eng="dve")
            cp(TOT, ps2[:, 0:NCH * 8], [pb2], [bb_], eng="dve")
            tt(ec1v, gv[:, :, 0:8], bv, ALU.subtract, [bg, bb_], [bb_])
            tt(WWt, TOT, EC1, ALU.add, [bb_], [bb_])
            act(WWt, WWt, AF.Exp, [bb_, b_small], [bb_], bias=LNSC)
            act(EC1, EC1, AF.Exp, [bb_], [bb_])
            act(ENB, Bm, AF.Exp, [bb_], [bb_], scale=-1.0)
            act(DEC, TOT, AF.Exp, [bb_], [bb_])
            if STOP <= 2:
                return bail()
            MSK = AR.bf(256)
            bm = Buf()
            ts(MSK[:, 0:128], cq("tri_le"), 128.0 ** -0.5, None, ALU.mult, None, [b_csq], [bm])
            ts(MSK[:, 128:256], cq("tri_ge"), 128.0 ** -0.5, None, ALU.mult, None, [b_csq], [bm])
            order = [list(range(NCH)), [1, 0] + list(range(NCH - 1, 1, -1))]
            QT = AR.bf(NT); KT = AR.bf(NT); KTM = AR.bf(NT); VA = AR.bf(NCH * 130)
            vav = VA.rearrange("p (c j) -> p c j", j=130)
            PPr = [AR.bf(128) for _ in range(6)]
            PPb = [Buf() for _ in range(6)]
            HD = [AR.f32(NCH * 130), AR.f32(NCH * 130)]
            DCB = AR.f32(2 * NCH)
            CST = [AR.f32(130), AR.f32(130)]
            CBF = [[AR.bf(130) for _ in range(3)] for _ in range(2)]
            KW = [AR.bf(128) for _ in range(4)]
            KWb = [Buf() for _ in range(4)]
            DCC = [AR.f32(2) for _ in range(4)]
            DCb = [Buf() for _ in range(4)]
            SS = AR.f32(NCH)
            OG = [AR.f32(128), AR.f32(128)]
            OGb = [Buf(), Buf()]
            YTM = [AR.bf(128), AR.bf(128)]
            YTMb = [Buf(), Buf()]
            HT = [AR.f32(128), AR.f32(128)]
            HTb = [Buf(), Buf()]
            kwq = 0
            dq = 0
            if STOP <= 2.5:
                return bail()
            def head_gen(h):
                nonlocal kwq
                sq_ = wpiece(); sk_ = wpiece(); sv_ = wpiece(); so_ = wpiece()
                bq, bk, bktm, bva, bh, bss = (Buf() for _ in range(6))
                bcst = [Buf(), Buf()]
                bcbf = [[Buf() for _ in range(3)] for _ in range(2)]
                for t, (a, b) in enumerate(TT):
                    ps, pb = next_ps()
                    proj_fm(sq_, t, ps, pb)
                    cp(QT[:, a:b], ps[:, :b - a], [pb], [bq])
                    ps, pb = next_ps()
                    proj_fm(sk_, t, ps, pb)
                    cp(KT[:, a:b], ps[:, :b - a], [pb], [bk], eng="dve")
                P.op("dve", lambda e: e.memset(VA, 1.0), (), [bva])
                for ch in range(NCH):
                    ps, pb = next_ps()
                    proj_tm(sk_, ch, ps, pb, 128)
                    cp(KTM[:, ch * 128:(ch + 1) * 128], ps[:, 0:128], [pb], [bktm])
                    psv_, pbv_ = next_ps()
                    proj_tm(sv_, ch, psv_, pbv_, 128)
                    cp(VA[:, ch * 130:ch * 130 + 128], psv_[:, 0:128], [pbv_], [bva], eng="dve")
                yield
                bhd = [[Buf() for _ in range(NCH)] for _ in range(2)]
                for d in range(2):
                    P.op("dve", (lambda d: lambda e: e.memset(CST[d][:, :], 0.0))(d), (), [bcst[d]])
                    P.op("dve", (lambda d: lambda e: e.memset(CBF[d][0][:, :], 0.0))(d), (), [bcbf[d][0]])
                ppq = 0
                pend = []

                def emit_acc(item):
                    d, ch, cs, kq, cur = item
                    ps, pb = next_ps()
                    mm(ps[:, 0:129], PPr[kq][:, :], vav[:, ch, 0:129], True, False, [PPb[kq], bva], [pb])
                    mm(ps[:, 0:129], QT[:, cs], CBF[d][cur][:, 0:129], False, True, [bq, bcbf[d][cur]], [pb])
                    cp(HD[d][:, ch * 130:ch * 130 + 129], ps[:, 0:129], [pb], [bhd[d][ch]])

                for step in range(NCH):
                    new_items = []
                    for d in range(2):
                        ch = order[d][step]
                        col = d * 4 + h
                        cs = slice(ch * 128, (ch + 1) * 128)
                        cur = step % 3
                        nxt = (step + 1) % 3
                        pss, pbs = next_ps()
                        mm(pss[:, 0:128], KT[:, cs], QT[:, cs], True, True, [bk, bq], [pbs])
                        kq = ppq % 6
                        ppq += 1
                        stt(PPr[kq][:, :], pss[:, 0:128], ec1v[:, ch, col:col + 1], MSK[:, d * 128:(d + 1) * 128],
                            ALU.mult, ALU.mult, [pbs, bb_, bm], [PPb[kq]])
                        new_items.append((d, ch, cs, kq, cur))
                        if step < NCH - 1:
                            k = kwq % 4
                            kwq += 1
                            ts(KW[k][:, :], KTM[:, cs], wwv[:, ch, col:col + 1], None, ALU.mult, None, [bktm, bb_], [KWb[k]])
                            ps2, pb2 = next_ps()
                            mm(ps2[:, 0:129], KW[k][:, :], vav[:, ch, 0:129], True, True, [KWb[k], bva], [pb2])
                            stt(CST[d][:, 0:129], CST[d][:, 0:129], decv[:, ch, col:col + 1], ps2[:, 0:129], ALU.mult, ALU.add,
                                [bcst[d], bb_, pb2], [bcst[d]])
                            cp(CBF[d][nxt][:, 0:129], CST[d][:, 0:129], [bcst[d]], [bcbf[d][nxt]])
                    for item in pend:
                        emit_acc(item)
                    pend = new_items
                for item in pend:
                    emit_acc(item)
                yield
                bdc = Buf()
                for d in range(2):
                    col = d * 4 + h
                    DN = HD[d][:, 128:NCH * 130:130]
                    dc = DCB[:, d * NCH:(d + 1) * NCH]
                    ts(dc, DN, -1.0, None, ALU.mult, None, bhd[d], [bdc])
                    tt(dc, dc, ENB[:, col * NCH:(col + 1) * NCH], ALU.max, [bdc, bb_], [bdc])
                    tt(dc, dc, DN, ALU.max, [bdc] + bhd[d], [bdc])
                    recip(dc, dc, [bdc], [bdc])
                    for ch in range(NCH):
                        ts(HD[d][:, ch * 130:ch * 130 + 128], HD[d][:, ch * 130:ch * 130 + 128], dc[:, ch:ch + 1], None,
                           ALU.mult, None, [bhd[d][ch], bdc], [bhd[d][ch]])
                tt(HD[0], HD[0], HD[1], ALU.add, bhd[0] + bhd[1], [bh])
                HOUTc = lambda ch: HD[0][:, ch * 130:ch * 130 + 128]
                for ch in range(NCH):
                    cs = slice(ch * 128, (ch + 1) * 128)
                    j = ch % 2
                    P.op("act", (lambda ch, cs, j: lambda e: e.activation(HT[j][:, :], HOUTc(ch), AF.Square,
                                                                            accum_out=SS[:, ch:ch + 1]))(ch, cs, j),
                         [bh], [HTb[j], bss])
                act(SS[:, :], SS[:, :], AF.Sqrt, [bss, b_small], [bss], bias=EPSC, scale=1.0 / 128)
                recip(SS[:, :], SS[:, :], [bss], [bss])
                def fin_A(ch):
                    j = ch % 2
                    ps, pb = next_ps()
                    proj_tm(so_, ch, ps, pb, 128)
                    act(OG[j][:, :], ps[:, 0:128], AF.Sigmoid, [pb], [OGb[j]])
                    stt(HT[j][:, :], HOUTc(ch), SS[:, ch:ch + 1], ROWP[:, 16 + h * 128:16 + (h + 1) * 128], ALU.mult, ALU.mult,
                        [bh, bss, b_rowp], [HTb[j]])
                    tt(YTM[j][:, :], HT[j][:, :], OG[j][:, :], ALU.mult, [HTb[j], OGb[j]], [YTMb[j]])

                def fin_B(ch):
                    j = ch % 2
                    t = tile_of_chunk(ch)
                    ps3, pb3 = next_ps()
                    pt = ps3[:, 0:64].bitcast(BF16)
                    tr(pt, YTM[j][:, :], cq("ident", True), [YTMb[j], b_csqb], [pb3])
                    cp(Y[:, h * NT + ch * 128:h * NT + (ch + 1) * 128], pt, [pb3], [Yb[h][t]], eng="dve")
                fin_A(0)
                for ch in range(NCH):
                    if ch + 1 < NCH:
                        fin_A(ch + 1)
                    fin_B(ch)

            gens = [head_gen(h) for h in range(4)]

            def finish_gen(g_):
                for _ in g_:
                    pass
            next(gens[0])
            next(gens[0])
            for h in range(1, 4):
                next(gens[h])
                finish_gen(gens[h - 1])
                next(gens[h])
            finish_gen(gens[3])
            fold(Y, Yb)

        def attention():
            AR.reset()
            Y, Yb = new_Y()
            COS = AR.bf(SEQ); SIN = AR.bf(SEQ)
            bcs = Buf()
            P.wait_only("sp", [(e_, P.cnt[e_]) for e_ in ("pe", "act", "dve")])
            P.dma("sp", lambda e: e.dma_start(out=COS, in_=dr["cossin"][:, 0:SEQ]), "cs2", w=[bcs])
            P.dma("sp", lambda e: e.dma_start(out=SIN, in_=dr["cossin"][:, SEQ:2 * SEQ]), "cs2", w=[bcs])
            bcs.w = ("cs2", P.dcnt["cs2"])
            VT = AR.bf(NCH * 128)
            bvt = Buf()
            ONB = AR.bf(64)
            bon = Buf()
            P.op("dve", lambda e: e.memset(ONB[:, :], 1.0), (), [bon])
            ONR = AR.bf(128)
            P.op("dve", lambda e: e.memset(ONR[0:1, :], 1.0), (), [bon])
            sv = wpiece()
            for ch in range(NCH):
                ps, pb = next_ps()
                proj_tm(sv, ch, ps, pb, 128)
                cp(VT[:, ch * 128:(ch + 1) * 128], ps[:, 0:128], [pb], [bvt])
            QR = [AR.bf(NT), AR.bf(NT)]
            KZ = [AR.bf(NT), AR.bf(NT)]
            KR = KZ[0]
            VTP = AR.bf(NCH * 256)
            OPAD = AR.bf(256)
            P.op("dve", lambda e: e.memset(OPAD, 0.0), (), [bon])
            P.op("dve", lambda e: e.memset(OPAD[:, 0:64], 1.0), (), [bon])
            P.op("dve", lambda e: e.memset(OPAD[:, 192:256], 1.0), (), [bon])
            pt_off = AR.off
            PT = [AR.bf(512) for _ in range(6)]
            PTb = [Buf() for _ in range(6)]
            PTHb = [Buf() for _ in range(12)]
            rc_off = AR.off
            RC = [AR.f32(128), AR.f32(128)]
            RCb = [Buf(), Buf()]
            SQ = [AR.f32(512), ARENA[:, pt_off:pt_off + 512]]
            SQb = [Buf(), Buf()]
            QN = [AR.f32(512), ARENA[:, pt_off + 512:pt_off + 1024]]
            QNb = [Buf(), Buf()]
            T1 = [AR.f32(512), ARENA[:, pt_off + 1024:pt_off + 1536]]
            T1b = [Buf(), Buf()]
            QNB = [AR.bf(512), ARENA[:, rc_off:rc_off + 256].bitcast(BF16)]
            QNBb = [Buf(), Buf()]
            qq = [0]
            pq = [0]

            def nr_stageA(s, gain, dst, dstb, t):
                a, b = TT[t]
                W = b - a
                j = qq[0] % 2
                qq[0] += 1
                ps, pb = next_ps()
                proj_fm(s, t, ps, pb)
                sqb = SQ[j][:, 0:256].bitcast(BF16)
                act(sqb[:, :W], ps[:, :W], AF.Square, [pb], [SQb[j]])
                ps2, pb2 = next_ps()
                mm(ps2[:, :W], cq("bones", True), sqb[:, :W], True, True, [SQb[j], b_csqb], [pb2])
                act(SQ[j][:, :W], ps2[:, :W], AF.Ln, [pb2, b_small], [SQb[j]], bias=EPSC, scale=1.0)
                act(SQ[j][:, :W], SQ[j][:, :W], AF.Exp, [SQb[j]], [SQb[j]], scale=-0.5)
                if t == 0:
                    stt(dst[:, a:b], ps[:, :W], gain, SQ[j][:, :W], ALU.mult, ALU.mult, [pb, CUR[0].b_sp, SQb[j]], [dstb[t]])
                    return None
                stt(QN[j][:, :W], ps[:, :W], gain, SQ[j][:, :W], ALU.mult, ALU.mult, [pb, CUR[0].b_sp, SQb[j]], [QNb[j]])
                cp(QNB[j][:, :W], QN[j][:, :W], [QNb[j]], [QNBb[j]])
                return (j, dst, dstb, t)

            def nr_stageB(st_):
                if st_ is None:
                    return
                j, dst, dstb, t = st_
                a, b = TT[t]
                W = b - a
                ps3, pb3 = next_ps()
                mm(ps3[:, :W], cq("rt", True), QNB[j][:, :W], True, True, [QNBb[j], b_csqb], [pb3])
                la, lb = a - CTX, b - CTX
                tt(T1[j][:, :W], ps3[:, :W], SIN[:, la:lb], ALU.mult, [pb3, bcs], [T1b[j]])
                tt(QN[j][:, :W], QN[j][:, :W], COS[:, la:lb], ALU.mult, [QNb[j], bcs], [QNb[j]])
                tt(dst[:, a:b], QN[j][:, :W], T1[j][:, :W], ALU.add, [QNb[j], T1b[j]], [dstb[t]])

            def normrope_multi(jobs):
                prev = None
                for (s_, gain, dst, dstb) in jobs:
                    for t in range(len(TT)):
                        cur = nr_stageA(s_, gain, dst, dstb, t)
                        nr_stageB(prev)
                        prev = cur
                nr_stageB(prev)

            for g in range(2):
                P.fence()
                sk = wpiece()
                bkr = [Buf() for _ in TT]
                bqr = [[Buf() for _ in TT] for _ in range(2)]
                sq0 = wpiece()
                sq1 = wpiece()
                normrope_multi([(sk, KNG(), KR, bkr), (sq0, QNG(), QR[0], bqr[0]), (sq1, QNG(), QR[1], bqr[1])])
                bkz = Buf()
                cp(KZ[1][:, :], KR[:, :], bkr, [bkz], eng="dve")
                P.op("dve", lambda e: e.memset(KZ[0][64:128, :], 0.0), (), [bkz] + bkr)
                P.op("dve", lambda e: e.memset(KZ[1][0:64, :], 0.0), (), [bkz])
                bvp = Buf()
                P.op("dve", lambda e: e.memset(VTP, 0.0), (), [bvp])
                for ch_ in range(NCH):
                    for hf in range(2):
                        cp(VTP[:, (ch_ * 2 + hf) * 128 + hf * 64:(ch_ * 2 + hf) * 128 + hf * 64 + 64],
                           VT[:, ch_ * 128 + g * 64:ch_ * 128 + (g + 1) * 64], [bvt], [bvp], eng="dve")
                def keys_of(ch):
                    if ch < 2:
                        return [(0, None), (1, None)]
                    keys = []
                    if ch - 1 >= 2:
                        keys.append((ch - 1, "tri_ge"))
                    keys.append((ch, None))
                    if ch + 1 < NCH:
                        keys.append((ch + 1, "tri_le"))
                    return keys + [(0, None), (1, None)]

                def stage1(ch, pair):
                    t = tile_of_chunk(ch)
                    qs = slice(ch * 128, (ch + 1) * 128)
                    pts = []
                    for (kb, msk) in keys_of(ch):
                        ks = slice(kb * 128, (kb + 1) * 128)
                        ps, pb = next_ps()
                        for half in range(2):
                            mm(ps[:, half * 128:(half + 1) * 128], KZ[half][:, ks], QR[pair][:, qs], True, True,
                               [bkz, bqr[pair][t]], [pb])
                        hj = pq[0] % 12
                        pq[0] += 1
                        pth = PT[hj // 2][:, (hj % 2) * 256:(hj % 2 + 1) * 256]
                        act(pth, ps[:, 0:256], AF.Exp, [pb], [PTHb[hj]], scale=0.125)
                        if msk is not None:
                            for half in range(2):
                                tt(pth[:, half * 128:(half + 1) * 128], pth[:, half * 128:(half + 1) * 128], cq(msk, True), ALU.mult,
                                   [PTHb[hj], b_csqb], [PTHb[hj]])
                        pts.append((hj, pth, kb))
                    return pts

                def stage2(ch, pair, pts):
                    t = tile_of_chunk(ch)
                    chunk = g * 2 + pair
                    pn, pnb = next_ps()
                    nk = len(pts)
                    pd, pdb = next_ps()
                    for half in range(2):
                        for ki, (hj, pth, kb) in enumerate(pts):
                            mm(pn[:, 0:128], VTP[:, (kb * 2 + half) * 128:(kb * 2 + half + 1) * 128],
                               pth[:, half * 128:(half + 1) * 128], half == 0 and ki == 0, half == 1 and ki == nk - 1,
                               [bvp, PTHb[hj]], [pnb])
                    for half in range(2):
                        for ki, (hj, pth, kb) in enumerate(pts):
                            mm(pd[:, 0:128], OPAD[:, half * 128:(half + 1) * 128], pth[:, half * 128:(half + 1) * 128],
                               half == 0 and ki == 0, False, [bon, PTHb[hj]], [pdb])
                    mm(pd[:, 0:128], SINKB[0:1, chunk * 128:(chunk + 1) * 128], ONR[0:1, :], False, True,
                       [b_sinkb, bon], [pdb])
                    j2 = (ch * 2 + pair) % 2
                    act(RC[j2][:, :], pd[:, 0:128], AF.Ln, [pdb], [RCb[j2]])
                    act(RC[j2][:, :], RC[j2][:, :], AF.Exp, [RCb[j2]], [RCb[j2]], scale=-1.0)
                    tt(Y[:, chunk * NT + ch * 128:chunk * NT + (ch + 1) * 128], pn[:, 0:128], RC[j2][:, :], ALU.mult,
                       [pnb, RCb[j2]], [Yb[chunk][t]])

                P.fence()
                prev = None
                for ch in range(NCH):
                    for pair in range(2):
                        pts = stage1(ch, pair)
                        if prev is not None:
                            stage2(*prev)
                        prev = (ch, pair, pts)
                stage2(*prev)
            fold(Y, Yb)

        def snap(k):
            if not dbg_snap:
                return
            for kc in range(8):
                P.dma("sp", (lambda kc: lambda e: e.dma_start(out=dr["dbg"][k, :, kc, :], in_=X[:, kc, CTX:NT]))(kc), "out",
                      r=[XB[kc][t] for t in range(1, 5)])

        load_spt(0, sets[0])
        ada_slots_reset()
        for part in range(6):
            ada_part(sets[0], part)
        ada_finish(sets[0])
        for l in range(NL):
            CUR[0] = sets[l % 2]
            load_rowp(l)
            norm(0)
            if "ffn" not in dbg_skip:
                ffn(0, fence=False)
            else:
                pc[0] += 66
            if l == 0:
                snap(0)
            norm(1)
            if "rg" not in dbg_skip:
                rglru()
            else:
                pc[0] += 36
            if l == 0:
                snap(1)
            if "ml" not in dbg_skip:
                mlstm()
            else:
                pc[0] += 41
            if l == 0:
                snap(2)
            if "at" not in dbg_skip:
                attention()
            else:
                pc[0] += 31
            if l == 0:
                snap(3)
            norm(2)
            if l + 1 < NL:
                assert "ffn" not in dbg_skip
                Tn = sets[(l + 1) % 2]
                load_spt(l + 1, Tn)
                ffn(2, extra=lambda gi, Tn=Tn: ada_part(Tn, gi), pre=ada_slots_reset)
                ada_finish(Tn)
            elif "ffn" not in dbg_skip:
                ffn(2)
            else:
                pc[0] += 66
        assert pc[0] == NP, (pc[0], NP)
        for kc in range(8):
            P.dma("sp", (lambda kc: lambda e: e.dma_start(out=dr["yT"][:, kc, :], in_=X[:, kc, CTX:NT]))(kc), "out",
                  r=[XB[kc][t] for t in range(1, 5)])
        P.wait_only("sp", [("out", P.dcnt["out"])])

        with nc.Block() as block:
            def mk(engname):
                def body(e):
                    for waits, fn, inc in P.ops[engname]:
                        emb = None
                        if fn is not None and EMBED_WAIT and waits:
                            emb = waits[-1]
                            waits = waits[:-1]
                        for k, v in waits:
                            e.wait_ge(sems[k], v)
                        if fn is not None:
                            ins = fn(e)
                            if emb is not None:
                                ins._wait_ge(sems[emb[0]], emb[1])
                            ins.then_inc(sems[inc[0]], inc[1])
                return body
            block.tensor(mk("pe"))
            block.scalar(mk("act"))
            block.vector(mk("dve"))
            block.gpsimd(mk("pool"))
            block.sync(mk("sp"))
    return nc, P


def make_in_maps(inp, NL, cores):
    consts = make_consts()
    ws, adas, sp, rowp = build_host_streams(inp, NL)
    csq = np.ascontiguousarray(np.concatenate([consts[k] for k in CONST_SQ], axis=1))
    import ml_dtypes
    cossin = np.ascontiguousarray(np.concatenate([consts["cos"], consts["sin"]], axis=1)).astype(ml_dtypes.bfloat16)
    maps = []
    for b in cores:
        cat = np.concatenate([inp["ctx"][b], inp["x"][b]], axis=0)
        xT = np.ascontiguousarray(cat.reshape(NT, 8, 128).transpose(2, 1, 0))
        scin = np.zeros((128, 8, 2), np.float32)
        scin[:, :, 0] = fm_vec(inp["c"][b])
        scin[:, :, 1] = fm_vec(inp["c_ctx"])
        maps.append({"xT": xT, "scin": scin.reshape(128, 16), "ws": ws, "adas": adas, "sp": sp, "rowp": rowp,
                     "csq": csq, "cossin": cossin})
    return maps


def run(inp, NL=DEPTH, cores=tuple(range(8)), dbg_skip=(), trace=False, dbg_snap=False):
    inp = {k: np.asarray(v, dtype=np.float32) for k, v in inp.items()}
    nc, _ = build_program(NL, dbg_skip, dbg_snap)
    maps = make_in_maps(inp, NL, cores)
    res = run_bass_kernel_spmd(nc, maps, core_ids=list(range(len(cores))), trace=trace)
    outs = []
    for r in res.results:
        yT = r["yT"]
        outs.append(np.ascontiguousarray(yT.transpose(2, 1, 0)).reshape(SEQ, D_MODEL))
    return np.stack(outs), res


def kernel(**inputs):
    out, _ = run(inputs)
    return out.astype(np.float32)
```

```python
import contextlib
import math
import numpy as np
import concourse.bass as bass
import concourse.mybir as mybir
from concourse.bass_utils import run_bass_kernel_spmd

F32 = mybir.dt.float32
F32R = mybir.dt.float32r
BF16 = mybir.dt.bfloat16
ALU = mybir.AluOpType
AF = mybir.ActivationFunctionType

D_MODEL = 1024; SEQ = 2048; CTX = 256; NT = SEQ + CTX; DEPTH = 4; D_FF = 2816
NFC = D_FF // 128
D_RNN = 512; ML_W = 512; AT_W = 512; AT_KVW = 128
OFF_RG_X = 0; OFF_RG_G = 512; OFF_ML_Q = 1024; OFF_ML_K = 1536; OFF_ML_V = 2048; OFF_ML_O = 2560
OFF_ML_G = 3072; OFF_AT_Q = 3088; OFF_AT_K = 3600; OFF_AT_V = 3728; OFF_BR_G = 3856
EPS = 1e-6
TT = [(0, 256), (256, 768), (768, 1280), (1280, 1792), (1792, 2304)]
NCH = NT // 128
NSLOT = 7
EMBED_WAIT = True
PIECES_PER_LAYER = 132 + 12 + 17 + 7 + 72
NSP = 160
NROWP = 1040
FFN_GROUPS = [list(range(g, min(g + 4, NFC))) for g in range(0, NFC, 4)]


class Buf:
    __slots__ = ("w", "r")

    def __init__(self):
        self.w = None
        self.r = {}


class Prog:
    ENG = ("pe", "act", "dve", "pool", "sp")

    def __init__(self):
        self.ops = {e: [] for e in self.ENG}
        self.cnt = {e: 0 for e in self.ENG}
        self.seen = {e: {} for e in self.ENG}
        self.dcnt = {}

    def _waits(self, eng, r, w):
        deps = {}

        def add(tok):
            if tok is None:
                return
            k, v = tok
            if deps.get(k, 0) < v:
                deps[k] = v
        for b in r:
            add(b.w)
        for b in w:
            add(b.w)
            for k, v in b.r.items():
                add((k, v))
        out = []
        for k, v in deps.items():
            if k == eng and eng == "pe":
                continue
            if self.seen[eng].get(k, 0) < v:
                self.seen[eng][k] = v
                out.append((k, v))
        return out

    def op(self, eng, fn, r=(), w=()):
        waits = self._waits(eng, r, w)
        self.cnt[eng] += 1
        n = self.cnt[eng]
        self.ops[eng].append((waits, fn, (eng, 1)))
        for b in r:
            if b.r.get(eng, 0) < n:
                b.r[eng] = n
        for b in w:
            b.w = (eng, n)
            b.r = {}

    def dma(self, q, fn, semkey, r=(), w=()):
        waits = self._waits(q, r, w)
        self.dcnt[semkey] = self.dcnt.get(semkey, 0) + 16
        n = self.dcnt[semkey]
        self.ops[q].append((waits, fn, (semkey, 16)))
        for b in r:
            b.r[semkey] = n
        for b in w:
            b.w = (semkey, n)
            b.r = {}

    def fence(self):
        for e in ("pe", "act", "dve"):
            waits = []
            for o in ("pe", "act", "dve"):
                if o == e and e == "pe":
                    continue
                if self.seen[e].get(o, 0) < self.cnt[o]:
                    self.seen[e][o] = self.cnt[o]
                    waits.append((o, self.cnt[o]))
            if waits:
                self.ops[e].append((waits, None, None))

    def wait_only(self, eng, toks):
        waits = []
        for k, v in toks:
            if self.seen[eng].get(k, 0) < v:
                self.seen[eng][k] = v
                waits.append((k, v))
        if waits:
            self.ops[eng].append((waits, None, None))


def fm_piece(W, c0, ncols=128):
    blk = np.zeros((1024, 128), np.float32)
    blk[:, :ncols] = W[:, c0:c0 + ncols]
    return np.ascontiguousarray(blk.reshape(8, 128, 128).transpose(1, 0, 2)).reshape(128, 1024)


def fm_piece_cols(cols):
    blk = np.zeros((1024, 128), np.float32)
    blk[:, :cols.shape[1]] = cols
    return np.ascontiguousarray(blk.reshape(8, 128, 128).transpose(1, 0, 2)).reshape(128, 1024)


def fm_vec(v):
    return np.ascontiguousarray(v.reshape(-1, 128).T)


def make_consts():
    c = {}
    c["ident"] = np.eye(128, dtype=np.float32)
    c["ones"] = np.ones((128, 128), np.float32)
    bo = np.zeros((128, 128), np.float32)
    bo[:64, :64] = 1.0 / 64
    bo[64:, 64:] = 1.0 / 64
    c["bones"] = bo
    s = np.arange(128)[:, None]
    t = np.arange(128)[None, :]
    c["tri_le"] = (s <= t).astype(np.float32)
    c["tri_ge"] = (s >= t).astype(np.float32)
    R = np.zeros((64, 64), np.float32)
    for m in range(16):
        R[m, m + 16] = -1.0
        R[m + 16, m] = 1.0
        R[m + 32, m + 48] = -1.0
        R[m + 48, m + 32] = 1.0
    RT = np.zeros((128, 128), np.float32)
    RT[:64, :64] = R.T
    RT[64:, 64:] = R.T
    c["rt"] = RT
    L = SEQ
    rows = L // 64
    row = np.repeat(np.arange(rows), 64).astype(np.float32)
    col = np.tile(np.arange(64), rows).astype(np.float32)
    half = 32
    inv = (10000.0 ** (-np.arange(0, half, 2, dtype=np.float32) / half)).astype(np.float32)
    ar = row[:, None] * inv
    ac = col[:, None] * inv
    ang = np.concatenate([ar, ar, ac, ac], axis=-1).astype(np.float32)
    c["cos"] = np.ascontiguousarray(np.concatenate([np.cos(ang).T, np.cos(ang).T], 0)).astype(np.float32)
    c["sin"] = np.ascontiguousarray(np.concatenate([np.sin(ang).T, np.sin(ang).T], 0)).astype(np.float32)
    return c


CONST_SQ = ["ident", "ones", "bones", "tri_le", "tri_ge", "rt"]


def build_host_streams(inp, NL):
    pieces = []
    ada = []
    sp = np.zeros((NL, 128, NSP), np.float32)
    rowp = np.zeros((NL, 128, NROWP), np.float32)
    for l in range(NL):
        w_in = inp["w_in"][l]
        for j in range(72):
            ada.append(fm_piece(inp["ada_w"][l], j * 128))

        def ffn_pieces(i, ada_next=None):
            for gi, grp in enumerate(FFN_GROUPS):
                for f in grp:
                    pieces.append(fm_piece(inp["ffn_w1"][l, i], f * 128))
                    pieces.append(fm_piece(inp["ffn_w3"][l, i], f * 128))
                for f in grp:
                    pieces.append(np.ascontiguousarray(inp["ffn_w2"][l, i][f * 128:(f + 1) * 128, :]))

        def fold_pieces(j):
            for o in range(8):
                pieces.append(fm_piece(w_in, OFF_BR_G + j * 1024 + o * 128))
                wb = inp["w_branch"][l, j][:, o * 128:(o + 1) * 128]
                blk = np.zeros((128, 1024), np.float32)
                blk[:, :512] = wb.reshape(4, 128, 128).transpose(1, 0, 2).reshape(128, 512)
                pieces.append(blk)
            for o2 in range(8):
                pieces.append(fm_piece(inp["w_out"][l], o2 * 128))

        ffn_pieces(0)
        for c in range(4):
            pieces.append(fm_piece(w_in, OFF_RG_X + c * 128))
            pieces.append(fm_piece(w_in, OFF_RG_G + c * 128))
            blk = np.zeros((128, 1024), np.float32)
            for idx, (arr, d) in enumerate([(inp["rg_wa"], 0), (inp["rg_wi"], 0), (inp["rg_wa"], 1), (inp["rg_wi"], 1)]):
                blk[0:64, idx * 128:idx * 128 + 64] = arr[l, d, 2 * c]
                blk[64:128, idx * 128 + 64:idx * 128 + 128] = arr[l, d, 2 * c + 1]
            pieces.append(blk)
        fold_pieces(0)
        gc = np.zeros((1024, 16), np.float32)
        for d in range(2):
            for h in range(4):
                gc[:, d * 4 + h] = w_in[:, OFF_ML_G + (2 * d) * 4 + h]
                gc[:, 8 + d * 4 + h] = w_in[:, OFF_ML_G + (2 * d + 1) * 4 + h]
        pieces.append(fm_piece_cols(gc))
        for h in range(4):
            for off in (OFF_ML_Q, OFF_ML_K, OFF_ML_V, OFF_ML_O):
                pieces.append(fm_piece(w_in, off + h * 128))
        fold_pieces(1)
        pieces.append(fm_piece(w_in, OFF_AT_V))
        for g in range(2):
            kc_ = w_in[:, OFF_AT_K + g * 64:OFF_AT_K + (g + 1) * 64]
            pieces.append(fm_piece_cols(np.concatenate([kc_, kc_], 1)))
            for qc in range(2):
                pieces.append(fm_piece(w_in, OFF_AT_Q + (2 * g + qc) * 128))
        fold_pieces(2)
        ffn_pieces(1, ada_next=(l + 1 if l + 1 < NL else None))
        o = 0
        for i in range(3):
            sp[l, :, o:o + 8] = fm_vec(inp["norm_g"][l, i]); o += 8
        sp[l, :, o:o + 72] = fm_vec(inp["ada_b"][l]); o += 72
        for k in range(4):
            sp[l, :, o:o + 4] = fm_vec(inp["rg_conv_w"][l, k]); o += 4
        sp[l, :, o:o + 4] = fm_vec(inp["rg_conv_b"][l]); o += 4
        for nm in ("rg_ba", "rg_bi", "rg_lam"):
            for d in range(2):
                sp[l, :, o:o + 4] = fm_vec(inp[nm][l, d]); o += 4
        sp[l, :, o] = np.tile(inp["at_qn_g"][l], 2); o += 1
        sp[l, :, o] = np.tile(inp["at_kn_g"][l], 2); o += 1
        gb = inp["ml_gate_b"][l]
        g16 = np.zeros(16, np.float32)
        for d in range(2):
            for h in range(4):
                g16[d * 4 + h] = gb[(2 * d) * 4 + h]
                g16[8 + d * 4 + h] = gb[(2 * d + 1) * 4 + h]
        rowp[l, :, 0:16] = g16[None, :]
        rowp[l, :, 16:528] = inp["ml_norm_g"][l][None, :]
        for c in range(4):
            for p in range(128):
                pass
        sk = inp["at_sink"][l]
        rowp[l, :, 528:1040] = np.repeat(sk, 64)[None, :]
    return (np.stack(pieces), np.stack(ada), sp, rowp)


def build_program(NL, dbg_skip=(), dbg_snap=False):
    nc = bass.Bass("TRN2", target_bir_lowering=False)
    P = Prog()
    NP = NL * PIECES_PER_LAYER
    dr = {}
    dr["xT"] = nc.dram_tensor("xT", [128, 8, NT], F32, kind="ExternalInput").ap()
    dr["scin"] = nc.dram_tensor("scin", [128, 16], F32, kind="ExternalInput").ap()
    dr["ws"] = nc.dram_tensor("ws", [NP, 128, 1024], F32, kind="ExternalInput").ap()
    dr["adas"] = nc.dram_tensor("adas", [NL * 72, 128, 1024], F32, kind="ExternalInput").ap()
    dr["sp"] = nc.dram_tensor("sp", [NL, 128, NSP], F32, kind="ExternalInput").ap()
    dr["rowp"] = nc.dram_tensor("rowp", [NL, 128, NROWP], F32, kind="ExternalInput").ap()
    dr["csq"] = nc.dram_tensor("csq", [128, len(CONST_SQ) * 128], F32, kind="ExternalInput").ap()
    dr["cossin"] = nc.dram_tensor("cossin", [128, 2 * SEQ], BF16, kind="ExternalInput").ap()
    dr["yT"] = nc.dram_tensor("yT", [128, 8, SEQ], F32, kind="ExternalOutput").ap()
    if dbg_snap:
        dr["dbg"] = nc.dram_tensor("dbg", [4, 128, 8, SEQ], F32, kind="ExternalOutput").ap()

    es = contextlib.ExitStack()
    with es:
        def sb(name, shape, dt):
            return es.enter_context(nc.sbuf_tensor(name, shape, dt))
        X = sb("X", [128, 8, NT], F32)
        H = sb("H", [128, 8, NT], BF16)
        WR = sb("WR", [128, NSLOT, 1024], BF16)
        ARENA_W = 18600
        ARENA = sb("ARENA", [128, ARENA_W], F32)
        CSQ = sb("CSQ", [128, len(CONST_SQ) * 128], F32)
        CSQB = sb("CSQB", [128, len(CONST_SQ) * 128], BF16)
        SPT2 = [sb("SPT0", [128, NSP], F32), sb("SPT1", [128, NSP], F32)]
        HB2 = [sb("HBA0", [128, 16], F32), sb("HBA1", [128, 16], F32)]
        ROWP = sb("ROWP", [128, NROWP], F32)
        SINKB = sb("SINKB", [1, 512], BF16)
        MODS2 = [sb("MODS0", [128, 72, 2], F32), sb("MODS1", [128, 72, 2], F32)]
        AM2 = [sb("AM0", [128, 3, 8, 2], F32), sb("AM1", [128, 3, 8, 2], F32)]
        GH2 = [sb("GH0", [128, 3, 8, 2], F32), sb("GH1", [128, 3, 8, 2], F32)]
        SC = sb("SC", [128, 8, 2], F32)
        SCB = sb("SCB", [128, 8, 2], BF16)
        SMALL = sb("SMALL", [128, 64], F32)
        PS = [es.enter_context(nc.psum_tensor("ps%d" % i, [128, 512], F32)) for i in range(8)]
        PB = [Buf() for _ in range(8)]
        psi = [0]

        def next_ps():
            i = psi[0] % 8
            psi[0] += 1
            return PS[i], PB[i]

        semkeys = ["pe", "act", "dve", "pool", "xld", "cst", "cs2", "spl0", "spl1", "rwp", "out"] + [("w", s) for s in range(NSLOT)] + [("a", s) for s in range(12)]
        sems = {}
        for k in semkeys:
            nm = k if isinstance(k, str) else "%s%d" % k
            sems[k] = es.enter_context(nc.semaphore("s_" + nm))

        XB = [[Buf() for _ in TT] for _ in range(8)]
        HB = [[Buf() for _ in TT] for _ in range(8)]
        WB = [Buf() for _ in range(NSLOT)]
        b_csq, b_csqb, b_sp, b_rowp, b_sink, b_sinkb, b_mods, b_am, b_gh, b_sc, b_small = (Buf() for _ in range(11))
        b_sp2 = [Buf(), Buf()]; b_mods2 = [Buf(), Buf()]; b_am2 = [Buf(), Buf()]; b_gh2 = [Buf(), Buf()]
        class NS:
            pass
        sets = []
        for par in range(2):
            ns = NS()
            ns.HB_ = HB2[par]
            ns.SPT = SPT2[par]; ns.MODS = MODS2[par]; ns.AM = AM2[par]; ns.GH = GH2[par]
            ns.b_sp = b_sp2[par]; ns.b_mods = b_mods2[par]; ns.b_am = b_am2[par]; ns.b_gh = b_gh2[par]
            sets.append(ns)
        CUR = [sets[0]]
        b_cs = Buf()

        def cq(name, bf=False):
            i = CONST_SQ.index(name)
            return (CSQB if bf else CSQ)[:, i * 128:(i + 1) * 128]

        def mm(out, lhsT, rhs, start, stop, r, w):
            P.op("pe", lambda e: e.matmul(out, lhsT, rhs, start=start, stop=stop), r, w)

        def tr(out, in_, ident, r, w):
            P.op("pe", lambda e: e.transpose(out, in_, ident), r, w)

        def act(out, in_, func, r, w, bias=None, scale=None):
            kw = {}
            if bias is not None:
                kw["bias"] = bias
            if scale is not None:
                kw["scale"] = scale
            P.op("act", lambda e: e.activation(out, in_, func, **kw), r, w)

        def tt(out, a, b, op, r, w, eng="dve"):
            eng = "dve"
            P.op(eng, lambda e: e.tensor_tensor(out, a, b, op), r, w)

        def ts(out, a, s1, s2, op0, op1, r, w, eng="dve"):
            eng = "dve"
            if s2 is None:
                P.op(eng, lambda e: e.tensor_scalar(out, a, s1, None, op0), r, w)
            else:
                P.op(eng, lambda e: e.tensor_scalar(out, a, s1, s2, op0, op1), r, w)

        def stt(out, a, s, b, op0, op1, r, w):
            P.op("dve", lambda e: e.scalar_tensor_tensor(out, a, s, b, op0, op1), r, w)

        def recip(out, a, r, w):
            P.op("dve", lambda e: e.reciprocal(out, a), r, w)

        def scan(out, d0, d1, init, r, w):
            P.op("dve", lambda e: e.tensor_tensor_scan(out, d0, d1, init, ALU.mult, ALU.add), r, w)

        def cp(out, in_, r, w, eng="act"):
            if eng == "act":
                act(out, in_, AF.Copy, r, w)
            else:
                P.op("dve", lambda e: e.tensor_copy(out, in_), r, w)

        class Arena:
            def __init__(self):
                self.off = 0

            def reset(self):
                P.fence()
                self.off = 0

            def f32(self, n):
                a = ARENA[:, self.off:self.off + n]
                self.off += n
                assert self.off <= ARENA_W, self.off
                return a

            def bf(self, n):
                assert n % 2 == 0
                a = ARENA[:, self.off:self.off + n // 2].bitcast(BF16)
                self.off += n // 2
                assert self.off <= ARENA_W, self.off
                return a
        AR = Arena()

        pc = [0]
        sl = [0]

        def wpiece():
            i = pc[0]
            pc[0] += 1
            s = sl[0] % NSLOT
            sl[0] += 1
            P.dma("pool", lambda e: e.dma_start(out=WR[:, s, :], in_=dr["ws"][i]), ("w", s), w=[WB[s]])
            return s

        for kc in range(8):
            P.dma("sp" if kc % 2 == 0 else "act", (lambda kc: lambda e: e.dma_start(out=X[:, kc, :], in_=dr["xT"][:, kc, :]))(kc), "xld", w=XB[kc])
        for kc in range(8):
            for b in XB[kc]:
                b.w = ("xld", P.dcnt["xld"])
        P.dma("sp", lambda e: e.dma_start(out=CSQ[:, :], in_=dr["csq"]), "cst", w=[b_csq])
        P.dma("sp", lambda e: e.dma_start(out=SC[:, :, :].rearrange("p a b -> p (a b)"), in_=dr["scin"]), "cst", w=[b_sc])
        b_csq.w = ("cst", P.dcnt["cst"])
        b_sc.w = ("cst", P.dcnt["cst"])
        cp(CSQB[:, :], CSQ[:, :], [b_csq], [b_csqb], eng="dve")
        P.op("dve", lambda e: e.memset(SMALL[:, 0:1], EPS), (), [b_small])
        P.op("dve", lambda e: e.memset(SMALL[:, 1:2], 1.0), (), [b_small])
        P.op("dve", lambda e: e.memset(SMALL[:, 2:3], math.log(128.0 ** -0.5)), (), [b_small])
        P.op("dve", lambda e: e.memset(SMALL[:, 3:4], 0.25), (), [b_small])
        EPSC = SMALL[:, 0:1]
        ONEC = SMALL[:, 1:2]
        LNSC = SMALL[:, 2:3]
        QUARTC = SMALL[:, 3:4]
        b_scb = Buf()
        act(SCB[:, :, :], SC[:, :, :], AF.Silu, [b_sc], [b_scb])

        def load_spt(l, T):
            key = "spl%d" % (l % 2)
            P.dma("sp", lambda e: e.dma_start(out=T.SPT[:, :], in_=dr["sp"][l]), key, w=[T.b_sp])

        def load_rowp(l):
            P.dma("sp", lambda e: e.dma_start(out=ROWP[:, :], in_=dr["rowp"][l]), "rwp", w=[b_rowp])
            act(SINKB[:, :], ROWP[0:1, 528:1040], AF.Exp, [b_rowp], [b_sinkb])
        NG = lambda i: CUR[0].SPT[:, i * 8:(i + 1) * 8]
        CW = lambda k, c: CUR[0].SPT[:, 96 + k * 4 + c:97 + k * 4 + c]
        CBv = lambda c: CUR[0].SPT[:, 112 + c:113 + c]
        BA = lambda d, c: CUR[0].SPT[:, 116 + d * 4 + c:117 + d * 4 + c]
        BI = lambda d, c: CUR[0].SPT[:, 124 + d * 4 + c:125 + d * 4 + c]
        QNG = lambda: CUR[0].SPT[:, 140:141]
        KNG = lambda: CUR[0].SPT[:, 141:142]
        COEFf = lambda: CUR[0].SPT[:, 144:152]
        COEFH = lambda: CUR[0].SPT[:, 152:160]
        BAH = lambda d, c: CUR[0].HB_[:, d * 4 + c:d * 4 + c + 1]
        BIH = lambda d, c: CUR[0].HB_[:, 8 + d * 4 + c:8 + d * 4 + c + 1]

        NAS = 12
        ADA_OFF = 7000
        ada_state = {"q": 0, "bufs": None, "idx": 0}

        def ada_slots_reset():
            P.wait_only("pool", [(e_, P.cnt[e_]) for e_ in ("pe", "act", "dve")])
            ada_state["bufs"] = [Buf() for _ in range(NAS)]
            ada_state["q"] = 0

        def ada_part(T, part):
            ps, pb = next_ps()
            j0 = part * 12
            for jj in range(12):
                k = ada_state["q"] % NAS
                ada_state["q"] += 1
                i = ada_state["idx"]
                ada_state["idx"] += 1
                slot = ARENA[:, ADA_OFF + k * 512:ADA_OFF + (k + 1) * 512].bitcast(BF16)
                sb_ = ada_state["bufs"][k]
                P.dma("pool", (lambda slot, i: lambda e: e.dma_start(out=slot, in_=dr["adas"][i]))(slot, i), ("a", k), w=[sb_])
                for kc in range(8):
                    mm(ps[:, 2 * jj:2 * jj + 2], slot[:, kc * 128:(kc + 1) * 128], SCB[:, kc, :],
                       kc == 0, kc == 7, [sb_, b_scb], [pb])
            psv = ps[:, 0:24].rearrange("p (j n) -> p j n", n=2)
            for n in range(2):
                tt(T.MODS[:, j0:j0 + 12, n], psv[:, :, n], T.SPT[:, 24 + j0:24 + j0 + 12], ALU.add, [pb, T.b_sp], [T.b_mods])

        def ada_finish(T):
            for i in range(3):
                for n in range(2):
                    stt(T.AM[:, i, :, n], T.MODS[:, (3 * i + 1) * 8:(3 * i + 2) * 8, n], 1.0, T.SPT[:, i * 8:(i + 1) * 8],
                        ALU.add, ALU.mult, [T.b_mods, T.b_sp], [T.b_am])
                    ts(T.GH[:, i, :, n], T.MODS[:, (3 * i + 2) * 8:(3 * i + 3) * 8, n], 0.5 if i != 1 else 1.0, None, ALU.mult, None,
                       [T.b_mods], [T.b_gh])
            COEF = T.SPT[:, 144:152]
            act(COEF, T.SPT[:, 132:140], AF.Exp, [T.b_sp], [T.b_sp], scale=-1.0)
            act(COEF, COEF, AF.Ln, [T.b_sp], [T.b_sp], bias=ONEC)
            ts(COEF, COEF, -8.0, None, ALU.mult, None, [T.b_sp], [T.b_sp])
            ts(T.SPT[:, 152:160], COEF, 0.5, None, ALU.mult, None, [T.b_sp], [T.b_sp])
            ts(T.HB_[:, 0:16], T.SPT[:, 116:132], 0.5, None, ALU.mult, None, [T.b_sp], [T.b_sp])

        def shift_ap(i, kc, n):
            return CUR[0].MODS[:, (3 * i) * 8 + kc, n:n + 1]

        NORM_OFF = ARENA_W - 3072
        nrm = {"SQb": [Buf(), Buf()], "RSb": [Buf(), Buf()], "TMPb": [Buf(), Buf()]}

        def norm(i):
            SQ = [ARENA[:, NORM_OFF + k * 512:NORM_OFF + (k + 1) * 512] for k in range(2)]
            SQb = nrm["SQb"]
            RS = [ARENA[:, NORM_OFF + (2 + k) * 512:NORM_OFF + (3 + k) * 512] for k in range(2)]
            RSb = nrm["RSb"]
            TMP = [ARENA[:, NORM_OFF + (4 + k) * 512:NORM_OFF + (5 + k) * 512] for k in range(2)]
            TMPb = nrm["TMPb"]
            st = {"q": 0}

            def stats(t):
                a, b = TT[t]
                W = b - a
                ps, pb = next_ps()
                for kc in range(8):
                    j = st["q"] % 2
                    st["q"] += 1
                    sqb = SQ[j][:, 0:256].bitcast(BF16)
                    if kc % 2 == 0:
                        act(sqb[:, :W], X[:, kc, a:b], AF.Square, [XB[kc][t]], [SQb[j]])
                    else:
                        tt(sqb[:, :W], X[:, kc, a:b], X[:, kc, a:b], ALU.mult, [XB[kc][t]], [SQb[j]])
                    mm(ps[:, :W], cq("ones", True), sqb[:, :W], kc == 0, kc == 7, [SQb[j], b_csqb], [pb])
                j = t % 2
                act(RS[j][:, :W], ps[:, :W], AF.Ln, [pb, b_small], [RSb[j]], bias=EPSC, scale=1.0 / D_MODEL)
                act(RS[j][:, :W], RS[j][:, :W], AF.Exp, [RSb[j]], [RSb[j]], scale=-0.5)

            def modulate(t):
                a, b = TT[t]
                W = b - a
                n = 1 if t == 0 else 0
                j = t % 2
                for kc in range(8):
                    jj = st["q"] % 2
                    st["q"] += 1
                    stt(TMP[jj][:, :W], X[:, kc, a:b], CUR[0].AM[:, i, kc, n:n + 1], RS[j][:, :W], ALU.mult, ALU.mult,
                        [XB[kc][t], CUR[0].b_am, RSb[j]], [TMPb[jj]])
                    act(H[:, kc, a:b], TMP[jj][:, :W], AF.Identity, [TMPb[jj], CUR[0].b_mods], [HB[kc][t]],
                        bias=shift_ap(i, kc, n), scale=1.0)
            stats(0)
            for t in range(len(TT)):
                if t + 1 < len(TT):
                    stats(t + 1)
                modulate(t)

        def Hall(t):
            return [HB[kc][t] for kc in range(8)]

        def proj_fm(s, t, ps, pb, M=128):
            a, b = TT[t]
            for kc in range(8):
                mm(ps[:M, :b - a], WR[:, s, kc * 128:kc * 128 + M], H[:, kc, a:b], kc == 0, kc == 7,
                   [WB[s], HB[kc][t]], [pb])

        def tile_of_chunk(ch):
            tok = ch * 128
            for t, (a, b) in enumerate(TT):
                if a <= tok < b:
                    return t

        def proj_tm(s, ch, ps, pb, ncols, col0=0):
            t = tile_of_chunk(ch)
            for kc in range(8):
                mm(ps[:, col0:col0 + ncols], H[:, kc, ch * 128:(ch + 1) * 128], WR[:, s, kc * 128:kc * 128 + ncols],
                   kc == 0, kc == 7, [WB[s], HB[kc][t]], [pb])

        ffn_b = {"Gb": [[Buf() for _ in TT] for _ in range(4)], "SILb": [Buf(), Buf()]}

        def ffn(i, extra=None, pre=None, fence=True):
            if fence:
                AR.reset()
            else:
                AR.off = 0
            if pre is not None:
                pre()
            G = AR.bf(4 * NT)
            Gb = ffn_b["Gb"]
            SIL = [AR.f32(512), AR.f32(512)]
            SILb = ffn_b["SILb"]
            q = 0
            for grp in FFN_GROUPS:
                for fi, f in enumerate(grp):
                    s1 = wpiece()
                    s3 = wpiece()
                    for t, (a, b) in enumerate(TT):
                        W = b - a
                        pa, pab = next_ps()
                        proj_fm(s1, t, pa, pab)
                        pb_, pbb = next_ps()
                        proj_fm(s3, t, pb_, pbb)
                        j = q % 2
                        q += 1
                        act(SIL[j][:, :W], pa[:, :W], AF.Silu, [pab], [SILb[j]])
                        tt(G[:, fi * NT + a:fi * NT + b], SIL[j][:, :W], pb_[:, :W], ALU.mult, [SILb[j], pbb], [Gb[fi][t]])
                s2 = [wpiece() for _ in grp]
                for t, (a, b) in enumerate(TT):
                    n = 1 if t == 0 else 0
                    W = b - a
                    for o in range(8):
                        ps, pb = next_ps()
                        for fi, f in enumerate(grp):
                            mm(ps[:, :W], WR[:, s2[fi], o * 128:(o + 1) * 128], G[:, fi * NT + a:fi * NT + b],
                               fi == 0, fi == len(grp) - 1, [WB[s2[fi]], Gb[fi][t]], [pb])
                        stt(X[:, o, a:b], ps[:, :W], CUR[0].GH[:, i, o, n:n + 1], X[:, o, a:b], ALU.mult, ALU.add,
                            [pb, CUR[0].b_gh, XB[o][t]], [XB[o][t]])
                if extra is not None:
                    extra(FFN_GROUPS.index(grp))

        def fold(Y, Yb):
            P.fence()
            AR.off = 4 * NT // 2
            MJ = AR.bf(8 * NT)
            MJb = [[Buf() for _ in TT] for _ in range(8)]
            SG = [AR.f32(512), AR.f32(512)]
            SGb = [Buf(), Buf()]
            q = 0
            for o in range(8):
                sg = wpiece()
                sbp = wpiece()
                for t, (a, b) in enumerate(TT):
                    W = b - a
                    pg, pgb = next_ps()
                    proj_fm(sg, t, pg, pgb)
                    pbr, pbrb = next_ps()
                    for kc in range(4):
                        mm(pbr[:, :W], WR[:, sbp, kc * 128:(kc + 1) * 128], Y[:, kc * NT + a:kc * NT + b], kc == 0, kc == 3,
                           [WB[sbp], Yb[kc][t]], [pbrb])
                    j = q % 2
                    q += 1
                    act(SG[j][:, :W], pg[:, :W], AF.Sigmoid, [pgb], [SGb[j]])
                    tt(MJ[:, o * NT + a:o * NT + b], SG[j][:, :W], pbr[:, :W], ALU.mult, [SGb[j], pbrb], [MJb[o][t]])
            for o2 in range(8):
                so = wpiece()
                for t, (a, b) in enumerate(TT):
                    n = 1 if t == 0 else 0
                    W = b - a
                    ps, pb = next_ps()
                    for o in range(8):
                        mm(ps[:, :W], WR[:, so, o * 128:(o + 1) * 128], MJ[:, o * NT + a:o * NT + b], o == 0, o == 7,
                           [WB[so], MJb[o][t]], [pb])
                    stt(X[:, o2, a:b], ps[:, :W], CUR[0].GH[:, 1, o2, n:n + 1], X[:, o2, a:b], ALU.mult, ALU.add,
                        [pb, CUR[0].b_gh, XB[o2][t]], [XB[o2][t]])

        def new_Y():
            Y = AR.bf(4 * NT)
            Yb = [[Buf() for _ in TT] for _ in range(4)]
            return Y, Yb

        def rglru():
            AR.reset()
            Y, Yb = new_Y()
            T1 = AR.f32(NT); T2 = AR.f32(NT); T3 = AR.f32(NT); T4 = AR.f32(NT); T5 = AR.f32(NT)
            UB = AR.bf(NT)
            nt_ = len(TT)
            b1 = [Buf() for _ in TT]; b2 = [Buf() for _ in TT]; b3 = [Buf() for _ in TT]
            b4 = [Buf() for _ in TT]; b5 = [Buf() for _ in TT]; bub = [Buf() for _ in TT]
            SEG = {0: (0, 256)}
            for t in range(1, nt_):
                SEG[t] = (256, NT)
            for c in range(4):
                sx = wpiece(); sgp = wpiece(); sw = wpiece()
                for t, (a, b) in enumerate(TT):
                    ps, pb = next_ps()
                    proj_fm(sx, t, ps, pb)
                    cp(T1[:, a:b], ps[:, :b - a], [pb], [b1[t]])
                for t, (a, b) in enumerate(TT):
                    s0, s1 = SEG[t]
                    nb_ = [b1[t]] + ([b1[t - 1]] if t - 1 >= 0 and SEG[t - 1] == SEG[t] else []) + \
                          ([b1[t + 1]] if t + 1 < nt_ and SEG[t + 1] == SEG[t] else [])
                    ts(T2[:, a:b], T1[:, a:b], CW(2, c), CBv(c), ALU.mult, ALU.add, [b1[t], CUR[0].b_sp], [b2[t]])
                    for k, off in ((0, -2), (1, -1), (3, 1)):
                        da, db = max(a, s0 - off), min(b, s1 - off)
                        stt(T2[:, da:db], T1[:, da + off:db + off], CW(k, c), T2[:, da:db], ALU.mult, ALU.add,
                            nb_ + [CUR[0].b_sp, b2[t]], [b2[t]])
                    cp(UB[:, a:b], T2[:, a:b], [b2[t]], [bub[t]])
                for d in range(2):
                    A_ = T1; I_ = T3; S_ = T4
                    cf = COEFH()[:, d * 4 + c:d * 4 + c + 1]
                    for t, (a, b) in enumerate(TT):
                        W = b - a
                        ps, pb = next_ps()
                        mm(ps[:, :W], WR[:, sw, (2 * d) * 128:(2 * d + 1) * 128], UB[:, a:b], True, True, [WB[sw], bub[t]], [pb])
                        act(A_[:, a:b], ps[:, :W], AF.Tanh, [pb, CUR[0].b_sp], [b1[t]], bias=BAH(d, c), scale=0.5)
                        act(A_[:, a:b], A_[:, a:b], AF.Exp, [b1[t], CUR[0].b_sp], [b1[t]], scale=cf, bias=cf)
                        ps2, pb2 = next_ps()
                        mm(ps2[:, :W], WR[:, sw, (2 * d + 1) * 128:(2 * d + 2) * 128], UB[:, a:b], True, True, [WB[sw], bub[t]], [pb2])
                        act(I_[:, a:b], ps2[:, :W], AF.Tanh, [pb2, CUR[0].b_sp], [b3[t]], bias=BIH(d, c), scale=0.5)
                        stt(I_[:, a:b], I_[:, a:b], 1.0, T2[:, a:b], ALU.add, ALU.mult, [b3[t], b2[t]], [b3[t]])
                        act(S_[:, a:b], A_[:, a:b], AF.Square, [b1[t]], [b4[t]])
                    act(S_[:, :], S_[:, :], AF.Sqrt, b4 + [b_small], b4, bias=QUARTC, scale=-0.25)
                    for t, (a, b) in enumerate(TT):
                        tt(I_[:, a:b], I_[:, a:b], S_[:, a:b], ALU.mult, [b3[t], b4[t]], [b3[t]])
                    if d == 0:
                        for t, (a, b) in enumerate(TT):
                            init = 0.0 if t == 0 else T5[:, a - 1:a]
                            rr = [b1[t], b3[t]] + ([b5[t - 1]] if t > 0 else [])
                            scan(T5[:, a:b], A_[:, a:b], I_[:, a:b], init, rr, [b5[t]])
                    else:
                        scan(T4[:, 255::-1], A_[:, 255::-1], I_[:, 255::-1], 0.0, [b1[0], b3[0], b4[0]], [b4[0]])
                        prev_t = 0
                        prev_col = 0
                        for t in range(nt_ - 1, 0, -1):
                            a, b = TT[t]
                            scan(T4[:, b - 1:a - 1:-1], A_[:, b - 1:a - 1:-1], I_[:, b - 1:a - 1:-1], T4[:, prev_col:prev_col + 1],
                                 [b1[t], b3[t], b4[t], b4[prev_t]], [b4[t]])
                            prev_t = t
                            prev_col = a
                for t, (a, b) in enumerate(TT):
                    tt(T5[:, a:b], T5[:, a:b], T4[:, a:b], ALU.add, [b5[t], b4[t]], [b5[t]])
                for t, (a, b) in enumerate(TT):
                    W = b - a
                    ps, pb = next_ps()
                    proj_fm(sgp, t, ps, pb)
                    g_ = T1[:, a:b]
                    act(g_, ps[:, :W], AF.Square, [pb], [b1[t]])
                    ts(g_, g_, 0.044715, 1.0, ALU.mult, ALU.add, [b1[t]], [b1[t]])
                    tt(g_, g_, ps[:, :W], ALU.mult, [b1[t], pb], [b1[t]])
                    act(g_, g_, AF.Tanh, [b1[t]], [b1[t]], scale=0.7978845608028654)
                    stt(g_, g_, 1.0, ps[:, :W], ALU.add, ALU.mult, [b1[t], pb], [b1[t]])
                    stt(Y[:, c * NT + a:c * NT + b], g_, 0.5, T5[:, a:b], ALU.mult, ALU.mult, [b1[t], b5[t]], [Yb[c][t]])
            fold(Y, Yb)

        def mlstm():
            import os
            STOP = float(os.environ.get("ML_STOP", "9"))
            pc_start = pc[0]

            def bail():
                pc[0] = pc_start + 41
            AR.reset()
            Y, Yb = new_Y()
            GTM = AR.f32(NCH * 16)
            gv = GTM.rearrange("p (c j) -> p c j", j=16)
            bg = Buf()
            sgate = wpiece()
            for ch in range(NCH):
                ps, pb = next_ps()
                proj_tm(sgate, ch, ps, pb, 16)
                tt(gv[:, ch, :], ps[:, 0:16], ROWP[:, 0:16], ALU.add, [pb, b_rowp], [bg])
            if STOP <= 1:
                return bail()
            LF = AR.f32(NCH * 8); lfv = LF.rearrange("p (j c) -> p c j", j=8)
            Bm = AR.f32(NCH * 8); bv = Bm.rearrange("p (j c) -> p c j", j=8)
            TOT = AR.f32(NCH * 8); totv = TOT.rearrange("p (j c) -> p c j", j=8)
            EC1 = AR.f32(NCH * 8); ec1v = EC1.rearrange("p (j c) -> p c j", j=8)
            ENB = AR.f32(NCH * 8); enbv = ENB.rearrange("p (j c) -> p c j", j=8)
            WWt = AR.f32(NCH * 8); wwv = WWt.rearrange("p (j c) -> p c j", j=8)
            DEC = AR.f32(NCH * 8); decv = DEC.rearrange("p (j c) -> p c j", j=8)
            bl = Buf()
            act(lfv, gv[:, :, 8:16], AF.Exp, [bg], [bl], scale=-1.0)
            act(lfv, lfv, AF.Ln, [bl, b_small], [bl], bias=ONEC)
            ts(LF, LF, -1.0, None, ALU.mult, None, [bl], [bl])
            ps, pb = next_ps()
            mm(ps[:, 0:4 * NCH], cq("tri_le"), LF[:, 0:4 * NCH], True, True, [bl, b_csq], [pb])
            mm(ps[:, 4 * NCH:8 * NCH], cq("tri_ge"), LF[:, 4 * NCH:8 * NCH], True, True, [bl, b_csq], [pb])
            ps2, pb2 = next_ps()
            mm(ps2[:, 0:NCH * 8], cq("ones"), LF, True, True, [bl, b_csq], [pb2])
            bb_ = Buf()
            cp(Bm, ps[:, 0:NCH * 8], [pb], [bb_], eng="dve")
            cp(TOT, ps2[:, 0:NCH * 8], [pb2], [bb_], eng="dve")
            tt(ec1v, gv[:, :, 0:8], bv, ALU.subtract, [bg, bb_], [bb_])
            tt(WWt, TOT, EC1, ALU.add, [bb_], [bb_])
            act(WWt, WWt, AF.Exp, [bb_, b_small], [bb_], bias=LNSC)
            act(EC1, EC1, AF.Exp, [bb_], [bb_])
            act(ENB, Bm, AF.Exp, [bb_], [bb_], scale=-1.0)
            act(DEC, TOT, AF.Exp, [bb_], [bb_])
            if STOP <= 2:
                return bail()
            MSK = AR.bf(256)
            bm = Buf()
            ts(MSK[:, 0:128], cq("tri_le"), 128.0 ** -0.5, None, ALU.mult, None, [b_csq], [bm])
            ts(MSK[:, 128:256], cq("tri_ge"), 128.0 ** -0.5, None, ALU.mult, None, [b_csq], [bm])
            order = [list(range(NCH)), [1, 0] + list(range(NCH - 1, 1, -1))]
            QT = AR.bf(NT); KT = AR.bf(NT); KTM = AR.bf(NT); VA = AR.bf(NCH * 130)
            vav = VA.rearrange("p (c j) -> p c j", j=130)
            PPr = [AR.bf(128) for _ in range(6)]
            PPb = [Buf() for _ in range(6)]
            HD = [AR.f32(NCH * 130), AR.f32(NCH * 130)]
            DCB = AR.f32(2 * NCH)
            CST = [AR.f32(130), AR.f32(130)]
            CBF = [[AR.bf(130) for _ in range(3)] for _ in range(2)]
            KW = [AR.bf(128) for _ in range(4)]
            KWb = [Buf() for _ in range(4)]
            DCC = [AR.f32(2) for _ in range(4)]
            DCb = [Buf() for _ in range(4)]
            SS = AR.f32(NCH)
            OG = [AR.f32(128), AR.f32(128)]
            OGb = [Buf(), Buf()]
            YTM = [AR.bf(128), AR.bf(128)]
            YTMb = [Buf(), Buf()]
            HT = [AR.f32(128), AR.f32(128)]
            HTb = [Buf(), Buf()]
            kwq = 0
            dq = 0
            if STOP <= 2.5:
                return bail()
            def head_gen(h):
                nonlocal kwq
                sq_ = wpiece(); sk_ = wpiece(); sv_ = wpiece(); so_ = wpiece()
                bq, bk, bktm, bva, bh, bss = (Buf() for _ in range(6))
                bcst = [Buf(), Buf()]
                bcbf = [[Buf() for _ in range(3)] for _ in range(2)]
                for t, (a, b) in enumerate(TT):
                    ps, pb = next_ps()
                    proj_fm(sq_, t, ps, pb)
                    cp(QT[:, a:b], ps[:, :b - a], [pb], [bq])
                    ps, pb = next_ps()
                    proj_fm(sk_, t, ps, pb)
                    cp(KT[:, a:b], ps[:, :b - a], [pb], [bk], eng="dve")
                P.op("dve", lambda e: e.memset(VA, 1.0), (), [bva])
                for ch in range(NCH):
                    ps, pb = next_ps()
                    proj_tm(sk_, ch, ps, pb, 128)
                    cp(KTM[:, ch * 128:(ch + 1) * 128], ps[:, 0:128], [pb], [bktm])
                    psv_, pbv_ = next_ps()
                    proj_tm(sv_, ch, psv_, pbv_, 128)
                    cp(VA[:, ch * 130:ch * 130 + 128], psv_[:, 0:128], [pbv_], [bva], eng="dve")
                yield
                bhd = [[Buf() for _ in range(NCH)] for _ in range(2)]
                for d in range(2):
                    P.op("dve", (lambda d: lambda e: e.memset(CST[d][:, :], 0.0))(d), (), [bcst[d]])
                    P.op("dve", (lambda d: lambda e: e.memset(CBF[d][0][:, :], 0.0))(d), (), [bcbf[d][0]])
                ppq = 0
                pend = []

                def emit_acc(item):
                    d, ch, cs, kq, cur = item
                    ps, pb = next_ps()
                    mm(ps[:, 0:129], PPr[kq][:, :], vav[:, ch, 0:129], True, False, [PPb[kq], bva], [pb])
                    mm(ps[:, 0:129], QT[:, cs], CBF[d][cur][:, 0:129], False, True, [bq, bcbf[d][cur]], [pb])
                    cp(HD[d][:, ch * 130:ch * 130 + 129], ps[:, 0:129], [pb], [bhd[d][ch]])

                for step in range(NCH):
                    new_items = []
                    for d in range(2):
                        ch = order[d][step]
                        col = d * 4 + h
                        cs = slice(ch * 128, (ch + 1) * 128)
                        cur = step % 3
                        nxt = (step + 1) % 3
                        pss, pbs = next_ps()
                        mm(pss[:, 0:128], KT[:, cs], QT[:, cs], True, True, [bk, bq], [pbs])
                        kq = ppq % 6
                        ppq += 1
                        stt(PPr[kq][:, :], pss[:, 0:128], ec1v[:, ch, col:col + 1], MSK[:, d * 128:(d + 1) * 128],
                            ALU.mult, ALU.mult, [pbs, bb_, bm], [PPb[kq]])
                        new_items.append((d, ch, cs, kq, cur))
                        if step < NCH - 1:
                            k = kwq % 4
                            kwq += 1
                            ts(KW[k][:, :], KTM[:, cs], wwv[:, ch, col:col + 1], None, ALU.mult, None, [bktm, bb_], [KWb[k]])
                            ps2, pb2 = next_ps()
                            mm(ps2[:, 0:129], KW[k][:, :], vav[:, ch, 0:129], True, True, [KWb[k], bva], [pb2])
                            stt(CST[d][:, 0:129], CST[d][:, 0:129], decv[:, ch, col:col + 1], ps2[:, 0:129], ALU.mult, ALU.add,
                                [bcst[d], bb_, pb2], [bcst[d]])
                            cp(CBF[d][nxt][:, 0:129], CST[d][:, 0:129], [bcst[d]], [bcbf[d][nxt]])
                    for item in pend:
                        emit_acc(item)
                    pend = new_items
                for item in pend:
                    emit_acc(item)
                yield
                bdc = Buf()
                for d in range(2):
                    col = d * 4 + h
                    DN = HD[d][:, 128:NCH * 130:130]
                    dc = DCB[:, d * NCH:(d + 1) * NCH]
                    ts(dc, DN, -1.0, None, ALU.mult, None, bhd[d], [bdc])
                    tt(dc, dc, ENB[:, col * NCH:(col + 1) * NCH], ALU.max, [bdc, bb_], [bdc])
                    tt(dc, dc, DN, ALU.max, [bdc] + bhd[d], [bdc])
                    recip(dc, dc, [bdc], [bdc])
                    for ch in range(NCH):
                        ts(HD[d][:, ch * 130:ch * 130 + 128], HD[d][:, ch * 130:ch * 130 + 128], dc[:, ch:ch + 1], None,
                           ALU.mult, None, [bhd[d][ch], bdc], [bhd[d][ch]])
                tt(HD[0], HD[0], HD[1], ALU.add, bhd[0] + bhd[1], [bh])
                HOUTc = lambda ch: HD[0][:, ch * 130:ch * 130 + 128]
                for ch in range(NCH):
                    cs = slice(ch * 128, (ch + 1) * 128)
                    j = ch % 2
                    P.op("act", (lambda ch, cs, j: lambda e: e.activation(HT[j][:, :], HOUTc(ch), AF.Square,
                                                                            accum_out=SS[:, ch:ch + 1]))(ch, cs, j),
                         [bh], [HTb[j], bss])
                act(SS[:, :], SS[:, :], AF.Sqrt, [bss, b_small], [bss], bias=EPSC, scale=1.0 / 128)
                recip(SS[:, :], SS[:, :], [bss], [bss])
                def fin_A(ch):
                    j = ch % 2
                    ps, pb = next_ps()
                    proj_tm(so_, ch, ps, pb, 128)
                    act(OG[j][:, :], ps[:, 0:128], AF.Sigmoid, [pb], [OGb[j]])
                    stt(HT[j][:, :], HOUTc(ch), SS[:, ch:ch + 1], ROWP[:, 16 + h * 128:16 + (h + 1) * 128], ALU.mult, ALU.mult,
                        [bh, bss, b_rowp], [HTb[j]])
                    tt(YTM[j][:, :], HT[j][:, :], OG[j][:, :], ALU.mult, [HTb[j], OGb[j]], [YTMb[j]])

                def fin_B(ch):
                    j = ch % 2
                    t = tile_of_chunk(ch)
                    ps3, pb3 = next_ps()
                    pt = ps3[:, 0:64].bitcast(BF16)
                    tr(pt, YTM[j][:, :], cq("ident", True), [YTMb[j], b_csqb], [pb3])
                    cp(Y[:, h * NT + ch * 128:h * NT + (ch + 1) * 128], pt, [pb3], [Yb[h][t]], eng="dve")
                fin_A(0)
                for ch in range(NCH):
                    if ch + 1 < NCH:
                        fin_A(ch + 1)
                    fin_B(ch)

            gens = [head_gen(h) for h in range(4)]

            def finish_gen(g_):
                for _ in g_:
                    pass
            next(gens[0])
            next(gens[0])
            for h in range(1, 4):
                next(gens[h])
                finish_gen(gens[h - 1])
                next(gens[h])
            finish_gen(gens[3])
            fold(Y, Yb)

        def attention():
            AR.reset()
            Y, Yb = new_Y()
            COS = AR.bf(SEQ); SIN = AR.bf(SEQ)
            bcs = Buf()
            P.wait_only("sp", [(e_, P.cnt[e_]) for e_ in ("pe", "act", "dve")])
            P.dma("sp", lambda e: e.dma_start(out=COS, in_=dr["cossin"][:, 0:SEQ]), "cs2", w=[bcs])
            P.dma("sp", lambda e: e.dma_start(out=SIN, in_=dr["cossin"][:, SEQ:2 * SEQ]), "cs2", w=[bcs])
            bcs.w = ("cs2", P.dcnt["cs2"])
            VT = AR.bf(NCH * 128)
            bvt = Buf()
            ONB = AR.bf(64)
            bon = Buf()
            P.op("dve", lambda e: e.memset(ONB[:, :], 1.0), (), [bon])
            ONR = AR.bf(128)
            P.op("dve", lambda e: e.memset(ONR[0:1, :], 1.0), (), [bon])
            sv = wpiece()
            for ch in range(NCH):
                ps, pb = next_ps()
                proj_tm(sv, ch, ps, pb, 128)
                cp(VT[:, ch * 128:(ch + 1) * 128], ps[:, 0:128], [pb], [bvt])
            QR = [AR.bf(NT), AR.bf(NT)]
            KZ = [AR.bf(NT), AR.bf(NT)]
            KR = KZ[0]
            VTP = AR.bf(NCH * 256)
            OPAD = AR.bf(256)
            P.op("dve", lambda e: e.memset(OPAD, 0.0), (), [bon])
            P.op("dve", lambda e: e.memset(OPAD[:, 0:64], 1.0), (), [bon])
            P.op("dve", lambda e: e.memset(OPAD[:, 192:256], 1.0), (), [bon])
            pt_off = AR.off
            PT = [AR.bf(512) for _ in range(6)]
            PTb = [Buf() for _ in range(6)]
            PTHb = [Buf() for _ in range(12)]
            rc_off = AR.off
            RC = [AR.f32(128), AR.f32(128)]
            RCb = [Buf(), Buf()]
            SQ = [AR.f32(512), ARENA[:, pt_off:pt_off + 512]]
            SQb = [Buf(), Buf()]
            QN = [AR.f32(512), ARENA[:, pt_off + 512:pt_off + 1024]]
            QNb = [Buf(), Buf()]
            T1 = [AR.f32(512), ARENA[:, pt_off + 1024:pt_off + 1536]]
            T1b = [Buf(), Buf()]
            QNB = [AR.bf(512), ARENA[:, rc_off:rc_off + 256].bitcast(BF16)]
            QNBb = [Buf(), Buf()]
            qq = [0]
            pq = [0]

            def nr_stageA(s, gain, dst, dstb, t):
                a, b = TT[t]
                W = b - a
                j = qq[0] % 2
                qq[0] += 1
                ps, pb = next_ps()
                proj_fm(s, t, ps, pb)
                sqb = SQ[j][:, 0:256].bitcast(BF16)
                act(sqb[:, :W], ps[:, :W], AF.Square, [pb], [SQb[j]])
                ps2, pb2 = next_ps()
                mm(ps2[:, :W], cq("bones", True), sqb[:, :W], True, True, [SQb[j], b_csqb], [pb2])
                act(SQ[j][:, :W], ps2[:, :W], AF.Ln, [pb2, b_small], [SQb[j]], bias=EPSC, scale=1.0)
                act(SQ[j][:, :W], SQ[j][:, :W], AF.Exp, [SQb[j]], [SQb[j]], scale=-0.5)
                if t == 0:
                    stt(dst[:, a:b], ps[:, :W], gain, SQ[j][:, :W], ALU.mult, ALU.mult, [pb, CUR[0].b_sp, SQb[j]], [dstb[t]])
                    return None
                stt(QN[j][:, :W], ps[:, :W], gain, SQ[j][:, :W], ALU.mult, ALU.mult, [pb, CUR[0].b_sp, SQb[j]], [QNb[j]])
                cp(QNB[j][:, :W], QN[j][:, :W], [QNb[j]], [QNBb[j]])
                return (j, dst, dstb, t)

            def nr_stageB(st_):
                if st_ is None:
                    return
                j, dst, dstb, t = st_
                a, b = TT[t]
                W = b - a
                ps3, pb3 = next_ps()
                mm(ps3[:, :W], cq("rt", True), QNB[j][:, :W], True, True, [QNBb[j], b_csqb], [pb3])
                la, lb = a - CTX, b - CTX
                tt(T1[j][:, :W], ps3[:, :W], SIN[:, la:lb], ALU.mult, [pb3, bcs], [T1b[j]])
                tt(QN[j][:, :W], QN[j][:, :W], COS[:, la:lb], ALU.mult, [QNb[j], bcs], [QNb[j]])
                tt(dst[:, a:b], QN[j][:, :W], T1[j][:, :W], ALU.add, [QNb[j], T1b[j]], [dstb[t]])

            def normrope_multi(jobs):
                prev = None
                for (s_, gain, dst, dstb) in jobs:
                    for t in range(len(TT)):
                        cur = nr_stageA(s_, gain, dst, dstb, t)
                        nr_stageB(prev)
                        prev = cur
                nr_stageB(prev)

            for g in range(2):
                P.fence()
                sk = wpiece()
                bkr = [Buf() for _ in TT]
                bqr = [[Buf() for _ in TT] for _ in range(2)]
                sq0 = wpiece()
                sq1 = wpiece()
                normrope_multi([(sk, KNG(), KR, bkr), (sq0, QNG(), QR[0], bqr[0]), (sq1, QNG(), QR[1], bqr[1])])
                bkz = Buf()
                cp(KZ[1][:, :], KR[:, :], bkr, [bkz], eng="dve")
                P.op("dve", lambda e: e.memset(KZ[0][64:128, :], 0.0), (), [bkz] + bkr)
                P.op("dve", lambda e: e.memset(KZ[1][0:64, :], 0.0), (), [bkz])
                bvp = Buf()
                P.op("dve", lambda e: e.memset(VTP, 0.0), (), [bvp])
                for ch_ in range(NCH):
                    for hf in range(2):
                        cp(VTP[:, (ch_ * 2 + hf) * 128 + hf * 64:(ch_ * 2 + hf) * 128 + hf * 64 + 64],
                           VT[:, ch_ * 128 + g * 64:ch_ * 128 + (g + 1) * 64], [bvt], [bvp], eng="dve")
                def keys_of(ch):
                    if ch < 2:
                        return [(0, None), (1, None)]
                    keys = []
                    if ch - 1 >= 2:
                        keys.append((ch - 1, "tri_ge"))
                    keys.append((ch, None))
                    if ch + 1 < NCH:
                        keys.append((ch + 1, "tri_le"))
                    return keys + [(0, None), (1, None)]

                def stage1(ch, pair):
                    t = tile_of_chunk(ch)
                    qs = slice(ch * 128, (ch + 1) * 128)
                    pts = []
                    for (kb, msk) in keys_of(ch):
                        ks = slice(kb * 128, (kb + 1) * 128)
                        ps, pb = next_ps()
                        for half in range(2):
                            mm(ps[:, half * 128:(half + 1) * 128], KZ[half][:, ks], QR[pair][:, qs], True, True,
                               [bkz, bqr[pair][t]], [pb])
                        hj = pq[0] % 12
                        pq[0] += 1
                        pth = PT[hj // 2][:, (hj % 2) * 256:(hj % 2 + 1) * 256]
                        act(pth, ps[:, 0:256], AF.Exp, [pb], [PTHb[hj]], scale=0.125)
                        if msk is not None:
                            for half in range(2):
                                tt(pth[:, half * 128:(half + 1) * 128], pth[:, half * 128:(half + 1) * 128], cq(msk, True), ALU.mult,
                                   [PTHb[hj], b_csqb], [PTHb[hj]])
                        pts.append((hj, pth, kb))
                    return pts

                def stage2(ch, pair, pts):
                    t = tile_of_chunk(ch)
                    chunk = g * 2 + pair
                    pn, pnb = next_ps()
                    nk = len(pts)
                    pd, pdb = next_ps()
                    for half in range(2):
                        for ki, (hj, pth, kb) in enumerate(pts):
                            mm(pn[:, 0:128], VTP[:, (kb * 2 + half) * 128:(kb * 2 + half + 1) * 128],
                               pth[:, half * 128:(half + 1) * 128], half == 0 and ki == 0, half == 1 and ki == nk - 1,
                               [bvp, PTHb[hj]], [pnb])
                    for half in range(2):
                        for ki, (hj, pth, kb) in enumerate(pts):
                            mm(pd[:, 0:128], OPAD[:, half * 128:(half + 1) * 128], pth[:, half * 128:(half + 1) * 128],
                               half == 0 and ki == 0, False, [bon, PTHb[hj]], [pdb])
                    mm(pd[:, 0:128], SINKB[0:1, chunk * 128:(chunk + 1) * 128], ONR[0:1, :], False, True,
                       [b_sinkb, bon], [pdb])
                    j2 = (ch * 2 + pair) % 2
                    act(RC[j2][:, :], pd[:, 0:128], AF.Ln, [pdb], [RCb[j2]])
                    act(RC[j2][:, :], RC[j2][:, :], AF.Exp, [RCb[j2]], [RCb[j2]], scale=-1.0)
                    tt(Y[:, chunk * NT + ch * 128:chunk * NT + (ch + 1) * 128], pn[:, 0:128], RC[j2][:, :], ALU.mult,
                       [pnb, RCb[j2]], [Yb[chunk][t]])

                P.fence()
                prev = None
                for ch in range(NCH):
                    for pair in range(2):
                        pts = stage1(ch, pair)
                        if prev is not None:
                            stage2(*prev)
                        prev = (ch, pair, pts)
                stage2(*prev)
            fold(Y, Yb)

        def snap(k):
            if not dbg_snap:
                return
            for kc in range(8):
                P.dma("sp", (lambda kc: lambda e: e.dma_start(out=dr["dbg"][k, :, kc, :], in_=X[:, kc, CTX:NT]))(kc), "out",
                      r=[XB[kc][t] for t in range(1, 5)])

        load_spt(0, sets[0])
        ada_slots_reset()
        for part in range(6):
            ada_part(sets[0], part)
        ada_finish(sets[0])
        for l in range(NL):
            CUR[0] = sets[l % 2]
            load_rowp(l)
            norm(0)
            if "ffn" not in dbg_skip:
                ffn(0, fence=False)
            else:
                pc[0] += 66
            if l == 0:
                snap(0)
            norm(1)
            if "rg" not in dbg_skip:
                rglru()
            else:
                pc[0] += 36
            if l == 0:
                snap(1)
            if "ml" not in dbg_skip:
                mlstm()
            else:
                pc[0] += 41
            if l == 0:
                snap(2)
            if "at" not in dbg_skip:
                attention()
            else:
                pc[0] += 31
            if l == 0:
                snap(3)
            norm(2)
            if l + 1 < NL:
                assert "ffn" not in dbg_skip
                Tn = sets[(l + 1) % 2]
                load_spt(l + 1, Tn)
                ffn(2, extra=lambda gi, Tn=Tn: ada_part(Tn, gi), pre=ada_slots_reset)
                ada_finish(Tn)
            elif "ffn" not in dbg_skip:
                ffn(2)
            else:
                pc[0] += 66
        assert pc[0] == NP, (pc[0], NP)
        for kc in range(8):
            P.dma("sp" if kc % 2 == 0 else "act", (lambda kc: lambda e: e.dma_start(out=dr["yT"][:, kc, :], in_=X[:, kc, CTX:NT]))(kc), "out",
                  r=[XB[kc][t] for t in range(1, 5)])
        P.wait_only("sp", [("out", P.dcnt["out"])])

        with nc.Block() as block:
            def mk(engname):
                def body(e):
                    for waits, fn, inc in P.ops[engname]:
                        emb = None
                        if fn is not None and EMBED_WAIT and waits:
                            emb = waits[-1]
                            waits = waits[:-1]
                        for k, v in waits:
                            e.wait_ge(sems[k], v)
                        if fn is not None:
                            ins = fn(e)
                            if emb is not None:
                                ins._wait_ge(sems[emb[0]], emb[1])
                            ins.then_inc(sems[inc[0]], inc[1])
                return body
            block.tensor(mk("pe"))
            block.scalar(mk("act"))
            block.vector(mk("dve"))
            block.gpsimd(mk("pool"))
            block.sync(mk("sp"))
    return nc, P


def make_in_maps(inp, NL, cores):
    consts = make_consts()
    ws, adas, sp, rowp = build_host_streams(inp, NL)
    csq = np.ascontiguousarray(np.concatenate([consts[k] for k in CONST_SQ], axis=1))
    import ml_dtypes
    cossin = np.ascontiguousarray(np.concatenate([consts["cos"], consts["sin"]], axis=1)).astype(ml_dtypes.bfloat16)
    maps = []
    for b in cores:
        cat = np.concatenate([inp["ctx"][b], inp["x"][b]], axis=0)
        xT = np.ascontiguousarray(cat.reshape(NT, 8, 128).transpose(2, 1, 0))
        scin = np.zeros((128, 8, 2), np.float32)
        scin[:, :, 0] = fm_vec(inp["c"][b])
        scin[:, :, 1] = fm_vec(inp["c_ctx"])
        maps.append({"xT": xT, "scin": scin.reshape(128, 16), "ws": ws, "adas": adas, "sp": sp, "rowp": rowp,
                     "csq": csq, "cossin": cossin})
    return maps


def run(inp, NL=DEPTH, cores=tuple(range(8)), dbg_skip=(), trace=False, dbg_snap=False):
    inp = {k: np.asarray(v, dtype=np.float32) for k, v in inp.items()}
    nc, _ = build_program(NL, dbg_skip, dbg_snap)
    maps = make_in_maps(inp, NL, cores)
    res = run_bass_kernel_spmd(nc, maps, core_ids=list(range(len(cores))), trace=trace)
    outs = []
    for r in res.results:
        yT = r["yT"]
        outs.append(np.ascontiguousarray(yT.transpose(2, 1, 0)).reshape(SEQ, D_MODEL))
    return np.stack(outs), res


def kernel(**inputs):
    out, _ = run(inputs)
    return out.astype(np.float32)
```
